# Optimizing a Trainium2 kernel written in Bass

```python
import jax, jax.numpy as jnp
from jax import lax
import numpy as np

D_MODEL = 1024
BATCH = 2
SEQ = 16384
DEPTH = 1
DEC_BATCH = 8
DEC_SEQ = 16
PAST_LEN = 2048

CHUNK = 64
D_MIX = D_MODEL
A_WIDTH = D_MIX // 2
A_GROUPS = 8
A_GDIM = A_WIDTH // A_GROUPS
MLP_CHUNK = 128
B_WIDTH = D_MIX - A_WIDTH
B_HEADS = 8
B_HDIM = B_WIDTH // B_HEADS
LEFT_CHUNKS = 8
KV_WIN = LEFT_CHUNKS * CHUNK
BAND = KV_WIN + CHUNK
REL_CLIP = 128
N_REL = 2 * REL_CLIP + 1
EPS = 1e-6
D_IN = 3 * A_WIDTH + 4 * B_WIDTH
SPLITS = [A_WIDTH, 2 * A_WIDTH, 3 * A_WIDTH, 3 * A_WIDTH + B_WIDTH,
          3 * A_WIDTH + 2 * B_WIDTH, 3 * A_WIDTH + 3 * B_WIDTH]
NEG = -1e30

kernel_name = "hymba_gmlp_chunkband_stream_step"


def rmsnorm(x, g):
    xf = x.astype(jnp.float32)
    y = xf * lax.rsqrt(jnp.mean(xf * xf, -1, keepdims=True) + EPS)
    return (y * g.astype(jnp.float32)).astype(x.dtype)


def layernorm(x, g, b):
    xf = x.astype(jnp.float32)
    mu = jnp.mean(xf, -1, keepdims=True)
    xc = xf - mu
    var = jnp.mean(xc * xc, -1, keepdims=True)
    y = xc * lax.rsqrt(var + EPS) * g.astype(jnp.float32) + b.astype(jnp.float32)
    return y.astype(x.dtype)


def rel_bias_lookup(rel_bias, dist):
    idx = jnp.clip(dist, -REL_CLIP, REL_CLIP) + REL_CLIP
    return jnp.take(rel_bias, idx, axis=1).astype(jnp.float32)


def mixer_inputs(x, c, g_pre, w_ada, b_ada, w_in, ln_g, ln_b):
    mod = jax.nn.silu(c) @ w_ada + b_ada
    shift, scale, gate = jnp.split(mod[:, None, :], 3, axis=-1)
    h = rmsnorm(x, g_pre) * (1 + scale) + shift
    z = h @ w_in
    uA, vA, gA, q, k, v, gB = jnp.split(z, SPLITS, axis=-1)
    uA = jax.nn.gelu(uA)
    vA = layernorm(jax.nn.gelu(vA), ln_g, ln_b)
    bsz, t = x.shape[0], x.shape[1]
    q = q.reshape(bsz, t, B_HEADS, B_HDIM)
    k = k.reshape(bsz, t, B_HEADS, B_HDIM)
    v = v.reshape(bsz, t, B_HEADS, B_HDIM)
    return gate, uA, vA, gA, q, k, v, gB


def mixer_output(x, gate, yA, gA, yB, gB, w_out, g_post):
    bsz, t = x.shape[0], x.shape[1]
    o = jnp.concatenate([yA * jax.nn.silu(gA),
                         yB.reshape(bsz, t, B_WIDTH) * jax.nn.silu(gB)], axis=-1) @ w_out
    return x + gate * rmsnorm(o, g_post)


def gmlp_prompt(u, v, w_s, b_s):
    bsz, s, _ = u.shape
    n = s // MLP_CHUNK
    vg = v.reshape(bsz, n, MLP_CHUNK, A_GROUPS, A_GDIM)
    mask = jnp.tril(jnp.ones((MLP_CHUNK, MLP_CHUNK), dtype=bool))
    ws = jnp.where(mask[None], w_s, jnp.zeros_like(w_s))
    mixed = jnp.einsum('gts,bnsgd->bntgd', ws, vg) + b_s.T[None, None, :, :, None]
    return u * mixed.reshape(bsz, s, A_WIDTH)


def gmlp_sample(u, v, w_s, b_s):
    bsz, t, _ = u.shape
    vg = v.reshape(bsz, t, A_GROUPS, A_GDIM)
    mask = jnp.tril(jnp.ones((t, t), dtype=bool))
    ws = w_s[:, :t, :t]
    ws = jnp.where(mask[None], ws, jnp.zeros_like(ws))
    mixed = jnp.einsum('gts,bsgd->btgd', ws, vg) + b_s[:, :t].T[None, :, :, None]
    return u * mixed.reshape(bsz, t, A_WIDTH)


def band_attention_prompt(q, k, v, rel_bias):
    bsz, s, nh, dh = q.shape
    n_chunks = s // CHUNK
    pad = ((0, 0), (KV_WIN, 0), (0, 0), (0, 0))
    kp = jnp.pad(k, pad)
    vp = jnp.pad(v, pad)
    kpos = jnp.arange(BAND)
    qpos = jnp.arange(CHUNK) + KV_WIN
    bias = rel_bias_lookup(rel_bias, qpos[:, None] - kpos[None, :])
    scale = dh ** -0.5

    def one_chunk(ci):
        start = ci * CHUNK
        qc = lax.dynamic_slice_in_dim(q, start, CHUNK, axis=1)
        kc = lax.dynamic_slice_in_dim(kp, start, BAND, axis=1)
        vc = lax.dynamic_slice_in_dim(vp, start, BAND, axis=1)
        valid = (kpos + start - KV_WIN) >= 0
        sc = jnp.einsum('bqhd,bkhd->bhqk', qc, kc,
                        preferred_element_type=jnp.float32) * scale + bias[None]
        sc = jnp.where(valid[None, None, None, :], sc, NEG)
        p = jax.nn.softmax(sc, axis=-1).astype(vc.dtype)
        return jnp.einsum('bhqk,bkhd->bqhd', p, vc)

    out = lax.map(one_chunk, jnp.arange(n_chunks))
    return out.transpose(1, 0, 2, 3, 4).reshape(bsz, s, nh, dh)


def band_attention_sample(q, k_new, v_new, k_cache, v_cache, rel_bias):
    w = k_cache.shape[1]
    t = q.shape[1]
    k = jnp.concatenate([k_cache.astype(k_new.dtype), k_new], axis=1)
    v = jnp.concatenate([v_cache.astype(v_new.dtype), v_new], axis=1)
    kpos = jnp.arange(w + t)
    qpos = w + jnp.arange(t)
    bias = rel_bias_lookup(rel_bias, qpos[:, None] - kpos[None, :])
    scale = q.shape[-1] ** -0.5
    sc = jnp.einsum('bqhd,bkhd->bhqk', q, k,
                    preferred_element_type=jnp.float32) * scale + bias[None]
    p = jax.nn.softmax(sc, axis=-1).astype(v.dtype)
    return jnp.einsum('bhqk,bkhd->bqhd', p, v)


def setup_inputs(seed: int = 0) -> dict:
    key = jax.random.key(seed)
    ks = jax.random.split(key, 20)
    f32 = jnp.float32
    win_rows = min(KV_WIN, PAST_LEN)
    nrm = lambda k_, shape: jax.random.normal(k_, shape, f32)
    return {
        "x_prompt": nrm(ks[0], (BATCH, SEQ, D_MODEL)),
        "x_sample": nrm(ks[1], (DEC_BATCH, DEC_SEQ, D_MODEL)),
        "cache_attn_k": nrm(ks[2], (DEPTH, DEC_BATCH, win_rows, B_HEADS, B_HDIM)),
        "cache_attn_v": nrm(ks[3], (DEPTH, DEC_BATCH, win_rows, B_HEADS, B_HDIM)),
        "c_prompt": nrm(ks[4], (BATCH, D_MODEL)),
        "c_sample": nrm(ks[5], (DEC_BATCH, D_MODEL)),
        "g_pre": 1.0 + 0.02 * nrm(ks[6], (DEPTH, D_MODEL)),
        "w_ada": 0.5 * D_MODEL ** -0.5 * nrm(ks[7], (DEPTH, D_MODEL, 3 * D_MODEL)),
        "b_ada": 0.02 * nrm(ks[8], (DEPTH, 3 * D_MODEL)),
        "w_in": D_MODEL ** -0.5 * nrm(ks[9], (DEPTH, D_MODEL, D_IN)),
        "ln_g": 1.0 + 0.02 * nrm(ks[10], (DEPTH, A_WIDTH)),
        "ln_b": 0.02 * nrm(ks[11], (DEPTH, A_WIDTH)),
        "w_s": MLP_CHUNK ** -0.5 * nrm(ks[12], (DEPTH, A_GROUPS, MLP_CHUNK, MLP_CHUNK)),
        "b_s": 1.0 + 0.02 * nrm(ks[13], (DEPTH, A_GROUPS, MLP_CHUNK)),
        "rel_bias": 0.1 * nrm(ks[14], (DEPTH, B_HEADS, N_REL)),
        "w_out": D_MIX ** -0.5 * nrm(ks[15], (DEPTH, D_MIX, D_MODEL)),
        "g_post": 1.0 + 0.02 * nrm(ks[16], (DEPTH, D_MODEL)),
    }


def reference(x_prompt, x_sample, cache_attn_k, cache_attn_v, c_prompt, c_sample,
              g_pre, w_ada, b_ada, w_in, ln_g, ln_b, w_s, b_s, rel_bias, w_out, g_post):
    yp = x_prompt
    ys = x_sample
    kp_rows, vp_rows, ks_rows, vs_rows, va_rows = [], [], [], [], []
    prompt_rows = min(KV_WIN, x_prompt.shape[1])
    for l in range(DEPTH):
        gate, uA, vA, gA, q, k, v, gB = mixer_inputs(yp, c_prompt, g_pre[l], w_ada[l], b_ada[l],
                                                     w_in[l], ln_g[l], ln_b[l])
        yA = gmlp_prompt(uA, vA, w_s[l], b_s[l])
        yB = band_attention_prompt(q, k, v, rel_bias[l])
        kp_rows.append(k[:, -prompt_rows:])
        vp_rows.append(v[:, -prompt_rows:])
        yp = mixer_output(yp, gate, yA, gA, yB, gB, w_out[l], g_post[l])

        gate, uA, vA, gA, q, k, v, gB = mixer_inputs(ys, c_sample, g_pre[l], w_ada[l], b_ada[l],
                                                     w_in[l], ln_g[l], ln_b[l])
        yA = gmlp_sample(uA, vA, w_s[l], b_s[l])
        yB = band_attention_sample(q, k, v, cache_attn_k[l], cache_attn_v[l], rel_bias[l])
        ks_rows.append(k)
        vs_rows.append(v)
        va_rows.append(vA)
        ys = mixer_output(ys, gate, yA, gA, yB, gB, w_out[l], g_post[l])

    new_k_prompt = jnp.stack(kp_rows)
    new_v_prompt = jnp.stack(vp_rows)
    new_k_sample = jnp.stack(ks_rows)
    new_v_sample = jnp.stack(vs_rows)
    new_gmlp_v_sample = jnp.stack(va_rows)
    return (yp, ys, new_k_prompt, new_v_prompt, new_k_sample, new_v_sample, new_gmlp_v_sample)
```

```python
import numpy as np
import contextlib
import concourse.bass as bass
import concourse.mybir as mybir
from concourse.bass_utils import run_bass_kernel_spmd

F32 = mybir.dt.float32
BF16 = mybir.dt.bfloat16
AF = mybir.ActivationFunctionType
ALU = mybir.AluOpType

P = 128
D = 1024
KT = 8
DIN = 3584
STK = 256
NTL = 2
RK = 8
EPS = 1e-6
BLK_U, BLK_VA, BLK_GA, BLK_Q, BLK_K, BLK_V, BLK_GB = range(7)


class Op:
    __slots__ = ("eng", "fn", "deps", "signal", "count", "chan", "idx")


class Prog:
    ENGS = ("pe", "act", "dve", "pool", "sp")

    def __init__(self):
        self.ops = {e: [] for e in self.ENGS}
        self.lastw = {}
        self.readers = {}
        self.chan_count = {}
        self.chan_last = {}

    def op(self, eng, fn, reads=(), writes=(), chan=None):
        o = Op()
        o.eng, o.fn, o.chan, o.signal, o.count = eng, fn, chan, False, None
        o.idx = self.nops = getattr(self, "nops", 0) + 1
        deps = set()
        for r in reads:
            w = self.lastw.get(r)
            if w is not None:
                deps.add(w)
        for w_ in writes:
            w = self.lastw.get(w_)
            if w is not None:
                deps.add(w)
            for rd in self.readers.get(w_, ()):
                deps.add(rd)
        deps.discard(o)
        best = {}
        for d in deps:
            if d.eng == "pe" and eng == "pe" and d.chan is None and chan is None:
                continue
            key = ("c", d.chan) if d.chan is not None else ("e", d.eng)
            if key not in best or best[key].idx < d.idx:
                best[key] = d
        o.deps = list(best.values())
        for d in o.deps:
            d.signal = True
        for r in reads:
            self.readers.setdefault(r, []).append(o)
        for w_ in writes:
            self.lastw[w_] = o
            self.readers[w_] = []
        if chan is not None:
            c = self.chan_count.get(chan, 0) + 16
            self.chan_count[chan] = c
            o.count = c
            self.chan_last[chan] = o
        self.ops[eng].append(o)
        return o

    def finalize(self):
        for e in self.ENGS:
            c = 0
            for o in self.ops[e]:
                if o.chan is None and o.signal:
                    c += 1
                    o.count = c

    def emit(self, eng_name, engine, eng_sems, chan_sems, final_waits=()):
        waited = {}
        for o in self.ops[eng_name]:
            need = {}
            for d in o.deps:
                if d.chan is not None:
                    key = ("c", d.chan)
                    sem = chan_sems[d.chan]
                else:
                    key = ("e", d.eng)
                    sem = eng_sems[d.eng]
                v = d.count
                if key not in need or need[key][1] < v:
                    need[key] = (sem, v)
            for key, (sem, v) in need.items():
                if waited.get(key, 0) < v:
                    engine.wait_ge(sem, v)
                    waited[key] = v
            inst = o.fn(engine)
            if o.chan is not None:
                inst.then_inc(chan_sems[o.chan], 16)
            elif o.signal:
                inst.then_inc(eng_sems[eng_name], 1)
        for sem, v in final_waits:
            engine.wait_ge(sem, v)


def build(NST=16, NHALO=2, sample=True):
    nc = bass.Bass("TRN2", target_bir_lowering=False)
    NTOK = NST * STK
    NHT = NHALO * STK
    LASTN = min(512, NTOK)

    def din(name, shape, dt=F32):
        return nc.dram_tensor(name, list(shape), dt, kind="ExternalInput").ap()

    def dout(name, shape, dt=F32):
        return nc.dram_tensor(name, list(shape), dt, kind="ExternalOutput").ap()

    xq = din("xq", [NTOK, D])
    xh = din("xh", [NHT, D])
    hval_d = din("hval", [P, 1])
    rmask_d = din("rmask", [P, 1])
    xs_d = din("xs", [P, D])
    ck_d = din("ck", [512, 512])
    cv_d = din("cv", [512, 512])
    cvec_d = din("cvec", [P, 16])
    wada_d = din("w_ada", [D, 3 * D])
    badac_d = din("b_ada_c", [P, 24])
    win_d = din("w_in", [D, DIN])
    gprec_d = din("g_pre_c", [P, 8])
    gpostc_d = din("g_post_c", [P, 8])
    lng_d = din("ln_g", [1, 512])
    lnb_d = din("ln_b", [1, 512])
    ws_d = din("w_s", [8, P, P])
    bsT_d = din("b_sT", [P, 8])
    rbp_d = din("rbp", [8, 384])
    rbl_d = din("rb_last", [1, 8])
    wout_d = din("w_out", [D, D])
    ident_d = din("ident", [P, P])
    antid_d = din("antiident", [P, P])
    tril_d = din("tril", [P, P])

    y_d = dout("y", [NTOK, D])
    ys_d = dout("ys", [P, D])
    nk_d = dout("nk", [LASTN, 512])
    nv_d = dout("nv", [LASTN, 512])
    nks_d = dout("nks", [P, 512])
    nvs_d = dout("nvs", [P, 512])
    gmv_d = dout("gmv", [P, 512])

    pg = Prog()
    es = contextlib.ExitStack()

    def sb(name, shape, dt):
        return es.enter_context(nc.sbuf_tensor("s_" + name, list(shape), dt))

    with es:
        w_in_bf = sb("w_in_bf", [P, KT, DIN], BF16)
        w_out_bf = sb("w_out_bf", [P, KT, D], BF16)
        arena = sb("arena", [P, 15360], BF16)
        ident_bf = sb("ident_bf", [P, P], BF16)
        ident_f = sb("ident_f", [P, P], F32)
        ones_f = sb("ones_f", [P, P], F32)
        dgs = [sb(f"dg{i}", [P, P], F32) for i in range(2)]
        cvec = sb("cvec", [P, 8, 2], F32)
        thc = sb("thc", [P, 8, 2], F32)
        sc_bf = sb("sc_bf", [P, 8, 2], BF16)
        badac = sb("badac", [P, 24], F32)
        gprec = sb("gprec", [P, 8], F32)
        gpostc = sb("gpostc", [P, 8], F32)
        mod = sb("mod", [P, 24, 2], F32)
        A2 = sb("A2", [P, 8, 2], F32)
        GC = sb("GC", [P, 8, 2], F32)
        GG = sb("GG", [P, D], F32)
        LGh = sb("LGh", [P, 512], F32)
        LBf = sb("LBf", [P, 512], F32)
        bsT = sb("bsT", [P, 8], F32)
        wsT = sb("wsT", [P, 8, P], BF16)
        Ch = sb("Ch", [P, 4, P], F32)
        negc = sb("negc", [P, 8], F32)
        E = sb("E", [P, 8, 3, P], BF16)
        hval = sb("hval_s", [P, 1], F32)
        rmask = sb("rmask_s", [P, 1], F32)
        mhalf = sb("mhalf", [P, 1], F32)
        dzero = sb("dzero", [P, 1], F32)
        dout_ = sb("dout", [P, 2], F32)
        NXIN = 2
        xin = [sb(f"xin{i}", [P, D], F32) for i in range(NXIN)]
        hn = [sb(f"hn{i}", [P, D], BF16) for i in range(2)]
        NXRES = 2
        xres = [sb(f"xres{i}", [P, D], F32) for i in range(NXRES)]
        tmpo = sb("tmpo", [P, 512], F32)
        hT = sb("hT", [P, KT, STK], BF16)
        qT2 = sb("qT2", [P, 4, 2, STK], BF16)
        kT = sb("kT", [P, 4, RK * P], BF16)
        V = sb("V", [P, RK, 8, 65], BF16)
        guT = sb("guT", [P, 4, STK], BF16)
        sgAT = sb("sgAT", [P, 4, STK], BF16)
        sgBT = sb("sgBT", [P, 4, STK], BF16)
        tht = [sb(f"tht{i}", [P, 512], BF16) for i in range(2)]
        gv = [sb(f"gv{i}", [P, 512], F32) for i in range(2)]
        vAn = [sb(f"vAn{i}", [P, 512], BF16) for i in range(2)]
        mixb = [sb(f"mixb{i}", [P, 512], BF16) for i in range(2)]
        tmpA = sb("tmpA", [P, 512], BF16)
        NPT = 5
        PT = [sb(f"PT{i}", [P, 2, 5, P], BF16) for i in range(NPT)]
        yBt = [sb(f"yBt{i}", [P, 512], BF16) for i in range(2)]
        o_inT = sb("o_inT", [P, KT, STK], BF16)
        NST4 = 8
        st4 = sb("st4", [P, NST4, 4], F32)
        bnst = [sb(f"bnst{i}", [P, 6], F32) for i in range(2)]
        bnmv = [sb(f"bnmv{i}", [P, 2], F32) for i in range(2)]
        rdt = [sb(f"rd{i}", [P, 4], F32) for i in range(2)]
        ps = [es.enter_context(nc.psum_tensor(f"ps{i}", [P, 512], F32)) for i in range(8)]

        def aview(lo, hi, dt=BF16, shape=None):
            v = arena[:, lo:hi]
            if dt is F32:
                v = v.bitcast(F32)
            if shape is not None:
                names = " ".join(f"a{i}" for i in range(len(shape)))
                kw = {f"a{i}": s for i, s in enumerate(shape)}
                v = v.rearrange(f"p ({names}) -> p {names}", **kw)
            return v
        wada_bf = [aview(0, 4096, BF16, (8, 512)), aview(4096, 8192, BF16, (8, 512))]
        Hk = aview(8192, 12288, F32, (2, 8, P))
        ws_f = aview(12288, 14336, F32, (8, P))
        tril_f = aview(14336, 14592, F32)
        J_f = aview(14592, 14848, F32)
        LBb = aview(14848, 15360, BF16)
        kTs = aview(0, 2560, BF16, (4, 5 * P))
        Vs = aview(2560, 5160, BF16, (5, 8, 65))
        ck_bf = aview(5160, 7208, BF16, (4, 512))
        vAn_f = aview(7208, 8232, F32)
        gmv_s = aview(8232, 9256, F32)
        nkst = aview(9256, 10280, F32)
        nvst = aview(10280, 11304, F32)
        GGs = aview(11304, 13352, F32)

        psn = [0]

        def newbank():
            b = psn[0] % 8
            psn[0] += 1
            return b

        def psr(b):
            return ("ps", b)

        def ps_bf(b):
            return ps[b][:, :].bitcast(BF16)

        def dma(eng, out, in_, reads, writes, chan, **kw):
            return pg.op(eng, lambda e, out=out, in_=in_, kw=kw: e.dma_start(out=out, in_=in_, **kw),
                         reads=reads, writes=writes, chan=chan)

        def bcast_rows(src, n):
            return bass.AP(tensor=src.tensor, offset=src.offset, ap=[[0, P], [1, n]])

        dma("sp", cvec[:, :, :], cvec_d.rearrange("p (k w) -> p k w", w=2), [], ["cvec"], "c_cvec")
        dma("sp", badac[:, :], badac_d, [], ["badac"], "c_badac")
        dma("sp", gprec[:, :], gprec_d, [], ["gprec"], "c_gprec")
        dma("sp", gpostc[:, :], gpostc_d, [], ["gpostc"], "c_gpostc")
        dma("sp", hval[:, :], hval_d, [], ["hval"], "c_hval")
        dma("sp", rmask[:, :], rmask_d, [], ["rmask"], "c_rmask")
        dma("sp", ident_f[:, :], ident_d, [], ["ident_f"], "c_identf")
        dma("sp", J_f, antid_d, [], ["J_f"], "c_J")
        dma("sp", tril_f, tril_d, [], ["tril"], "c_tril")
        dma("sp", bsT[:, :], bsT_d, [], ["bsT"], "c_bsT")
        dma("sp", LGh[:, :], bcast_rows(lng_d, 512), [], ["LGh"], "c_lg")
        dma("sp", LBf[:, :], bcast_rows(lnb_d, 512), [], ["LBf"], "c_lb")
        dma("sp", negc[:, :], bcast_rows(rbl_d, 8), [], ["negc"], "c_negc")
        dma("sp", ws_f, ws_d.rearrange("g t s -> t g s"), [], ["ws_f"], "c_ws")
        for w_, off in ((0, 129), (1, 1)):
            src = bass.AP(tensor=rbp_d.tensor, offset=off, ap=[[1, P], [384, 8], [1, P]])
            dma("sp", Hk[:, w_, :, :], src, [], [("Hk", w_)], f"c_hk{w_}")
        dma("pool", ident_bf[:, :], ident_d, [], ["ident_bf"], "c_identb")

        pool_ms = lambda out, val, writes, reads=(): pg.op(
            "pool", lambda e, out=out, val=val: e.memset(out, val), reads=reads, writes=writes)
        pool_ms(ones_f[:, :], 1.0, ["ones_f"])
        pool_ms(mhalf[:, :], -0.5, ["mhalf"])
        pool_ms(dzero[:, :], 0.0, ["dzero"])
        pool_ms(qT2[:, :, :, :], 0.0, [("qT", i) for i in range(4)])
        pool_ms(E[:, :, 0, :], 1.0, ["E0"])
        pool_ms(E[0:64, :, 0, 64:128], 0.0, ["E0"])

        pg.op("act", lambda e: e.activation(out=thc[:, :, :], in_=cvec[:, :, :], func=AF.Tanh, scale=0.5),
              reads=["cvec"], writes=["thc"])
        pg.op("dve", lambda e: e.scalar_tensor_tensor(out=thc[:, :, :], in0=thc[:, :, :], scalar=1.0,
                                                      in1=cvec[:, :, :], op0=ALU.add, op1=ALU.mult),
              reads=["thc", "cvec"], writes=["thc"])
        pg.op("dve", lambda e: e.tensor_scalar(out=sc_bf[:, :, :], in0=thc[:, :, :], scalar1=0.5,
                                               scalar2=None, op0=ALU.mult),
              reads=["thc"], writes=["sc_bf"])

        wada_i = [0]

        def wada_dma(ci):
            slot = wada_i[0] % 2
            wada_i[0] += 1
            dma("pool", wada_bf[slot],
                wada_d[:, ci * 512:(ci + 1) * 512].rearrange("(k p) c -> p k c", p=P),
                [], [("wada", slot)], f"c_wada{slot}")
            return slot

        def wada_mm(ci, slot, bank):
            for jj in range(4):
                j = ci * 4 + jj
                for k in range(KT):
                    pg.op("pe", lambda e, j=j, jj=jj, k=k, slot=slot, bank=bank: e.matmul(
                        ps[bank][:, 2 * j:2 * j + 2], lhsT=wada_bf[slot][:, k, jj * P:(jj + 1) * P],
                        rhs=sc_bf[:, k, :], start=(k == 0), stop=(k == KT - 1)),
                        reads=[("wada", slot), "sc_bf"], writes=[psr(bank)])

        def mod_chunks(chunks, bank):
            for ci in chunks:
                wada_mm(ci, wada_dma(ci), bank)

        bmod = newbank()
        mod_chunks([2, 3, 0, 1], bmod)
        pg.op("dve", lambda e: e.tensor_tensor(
            out=mod[:, 0:16, :], in0=ps[bmod][:, 0:32].rearrange("p (j w) -> p j w", w=2),
            in1=badac[:, 0:16].unsqueeze(2).to_broadcast([P, 16, 2]), op=ALU.add),
            reads=[psr(bmod), "badac"], writes=[("mod", 0)])
        pg.op("dve", lambda e: e.scalar_tensor_tensor(
            out=A2[:, :, :], in0=mod[:, 8:16, :], scalar=1.0,
            in1=gprec[:, :].unsqueeze(2).to_broadcast([P, 8, 2]), op0=ALU.add, op1=ALU.mult),
            reads=[("mod", 0), "gprec"], writes=["A2"])

        for c0, c1, blks in ((2048, 3072, (BLK_K, BLK_V)), (0, 1024, (BLK_U, BLK_VA)),
                             (1024, 2048, (BLK_GA, BLK_Q)), (3072, 3584, (BLK_GB,))):
            dma("pool", w_in_bf[:, :, c0:c1],
                win_d[:, c0:c1].rearrange("(k p) c -> p k c", p=P),
                [], [("win", b_) for b_ in blks], f"c_win{c0}")
        gate_slots = [wada_dma(4), wada_dma(5)]
        dma("pool", w_out_bf[:, :, :], wout_d.rearrange("(k p) c -> p k c", p=P),
            [], [("wout", 0), ("wout", 1)], "c_wout")

        def make_GG(w, dst=None, dres="GG"):
            dst = GG if dst is None else dst
            for hf in range(2):
                b = newbank()
                for kk in range(4):
                    k = hf * 4 + kk
                    dg = dgs[k % 2]
                    pg.op("dve", lambda e, dg=dg, k=k: e.tensor_scalar(
                        out=dg[:, :], in0=ident_f[:, :], scalar1=GC[:, k, w:w + 1], scalar2=None,
                        op0=ALU.mult), reads=["ident_f", "GC"], writes=[("dg", k % 2)])
                    pg.op("pe", lambda e, b=b, kk=kk, dg=dg: e.matmul(
                        ps[b][:, kk * P:(kk + 1) * P], lhsT=ones_f[:, :], rhs=dg[:, :],
                        start=True, stop=True),
                        reads=["ones_f", ("dg", k % 2)], writes=[psr(b)])
                pg.op("dve", lambda e, b=b, hf=hf, dst=dst: e.tensor_copy(
                    out=dst[:, hf * 512:(hf + 1) * 512], in_=ps[b][:, :]),
                    reads=[psr(b)], writes=[(dres, hf)])
        pg.op("dve", lambda e: e.tensor_copy(out=LBb, in_=LBf[:, :]), reads=["LBf"], writes=["LBb"])
        pg.op("dve", lambda e: e.tensor_tensor(
            out=ws_f, in0=ws_f, in1=tril_f.unsqueeze(1).to_broadcast([P, 8, P]), op=ALU.mult),
            reads=["ws_f", "tril"], writes=["ws_f"])
        for hf in range(2):
            b = newbank()
            for gg in range(4):
                g = hf * 4 + gg
                pg.op("pe", lambda e, b=b, gg=gg, g=g: e.transpose(
                    out=ps[b][:, gg * P:(gg + 1) * P], in_=ws_f[:, g, :], identity=ident_f[:, :]),
                    reads=["ws_f", "ident_f"], writes=[psr(b)])
            pg.op("dve", lambda e, b=b, hf=hf: e.tensor_copy(
                out=wsT[:, hf * 4:(hf + 1) * 4, :],
                in_=ps[b][:, :].rearrange("p (g t) -> p g t", t=P)),
                reads=[psr(b)], writes=["wsT"])
        b = newbank()
        for g in range(8):
            pg.op("pe", lambda e, b=b, g=g: e.matmul(
                ps[b][:, g * 64:(g + 1) * 64], lhsT=wsT[:, g, :], rhs=LBb[:, g * 64:(g + 1) * 64],
                start=True, stop=True), reads=["wsT", "LBb"], writes=[psr(b)])
        pg.op("dve", lambda e, b=b: e.tensor_tensor(
            out=gv[0][:, :].rearrange("p (g d) -> p g d", d=64),
            in0=ps[b][:, :].rearrange("p (g d) -> p g d", d=64),
            in1=bsT[:, :].unsqueeze(2).to_broadcast([P, 8, 64]), op=ALU.add),
            reads=[psr(b), "bsT"], writes=[("gv", 0)])
        pg.op("dve", lambda e: e.tensor_scalar(out=gv[0][:, :], in0=gv[0][:, :], scalar1=0.5,
                                               scalar2=None, op0=ALU.mult),
              reads=[("gv", 0)], writes=[("gv", 0)])
        b = newbank()
        for j in range(4):
            pg.op("pe", lambda e, b=b, j=j: e.transpose(
                out=ps[b][:, j * P:(j + 1) * P], in_=gv[0][:, j * P:(j + 1) * P],
                identity=ident_f[:, :]), reads=[("gv", 0), "ident_f"], writes=[psr(b)])
        pg.op("dve", lambda e, b=b: e.tensor_copy(
            out=Ch[:, :, :], in_=ps[b][:, :].rearrange("p (j t) -> p j t", t=P)),
            reads=[psr(b)], writes=["Ch"])
        pg.op("dve", lambda e: e.tensor_scalar(out=LGh[:, :], in0=LGh[:, :], scalar1=0.5,
                                               scalar2=None, op0=ALU.mult),
              reads=["LGh"], writes=["LGh"])
        pg.op("dve", lambda e: e.tensor_scalar(out=negc[:, :], in0=negc[:, :], scalar1=-1.0,
                                               scalar2=None, op0=ALU.mult),
              reads=["negc"], writes=["negc"])
        for w_ in range(2):
            for h0 in (0, 4):
                b = newbank()
                pg.op("pe", lambda e, b=b, w_=w_, h0=h0: e.matmul(
                    ps[b][:, :], lhsT=J_f, rhs=Hk[:, w_, h0:h0 + 4, :], start=True, stop=True),
                    reads=["J_f", ("Hk", w_)], writes=[psr(b)])
                for hh in range(4):
                    h = h0 + hh
                    pg.op("act", lambda e, b=b, hh=hh, h=h, w_=w_: e.activation(
                        out=E[:, h, 1 + w_, :], in_=ps[b][:, hh * P:(hh + 1) * P], func=AF.Exp,
                        bias=negc[:, h:h + 1], scale=1.0),
                        reads=[psr(b), "negc"], writes=[("E", 1 + w_)])
        pool_ms(E[64:128, :, 2, 0:64], 0.0, [("E", 2)])

        cnt = {"xin": 0, "hn": 0, "xres": 0, "st4": 0, "bn": 0, "gv": 0, "van": 0, "mix": 0,
               "pt": 0, "ybt": 0, "tht": 0, "rd": 0}

        def nxt(name, n):
            v = cnt[name] % n
            cnt[name] += 1
            return v

        def rsqrt_chain(src_ap, src_res, scale, st_slot):
            r = ("st4", st_slot)
            pg.op("dve", lambda e: e.tensor_scalar(
                out=st4[:, st_slot, 1:2], in0=src_ap, scalar1=scale, scalar2=EPS,
                op0=ALU.mult, op1=ALU.add), reads=[src_res], writes=[(r, 1)])
            pg.op("pool", lambda e: e.tensor_tensor(
                out=st4[:, st_slot, 2:3], in0=st4[:, st_slot, 1:2], in1=mhalf[:, :], op=ALU.pow),
                reads=[(r, 1), "mhalf"], writes=[(r, 2)])
            return (r, 2)

        def in1_dma(x_src_rows, ntile):
            xslots = []
            for t in range(ntile):
                xs_ = nxt("xin", NXIN)
                dma("sp", xin[xs_][:, :], x_src_rows(t), [], [("xin", xs_)], f"xin{xs_}")
                xslots.append(xs_)
            return xslots

        def in1_compute(xslots):
            hslots = []
            for xs_ in xslots:
                hs_ = nxt("hn", 2)
                s4 = nxt("st4", NST4)
                pg.op("act", lambda e, xs_=xs_, hs_=hs_, s4=s4: e.activation(
                    out=hn[hs_][:, :], in_=xin[xs_][:, :], func=AF.Square,
                    accum_out=st4[:, s4, 0:1]),
                    reads=[("xin", xs_)], writes=[("hn", hs_), (("st4", s4), 0)])
                rr = rsqrt_chain(st4[:, s4, 0:1], (("st4", s4), 0), 1.0 / D, s4)
                pg.op("pool", lambda e, xs_=xs_, hs_=hs_, s4=s4: e.tensor_scalar(
                    out=hn[hs_][:, :], in0=xin[xs_][:, :], scalar1=st4[:, s4, 2:3], scalar2=1.0,
                    op0=ALU.mult, op1=ALU.mult),
                    reads=[("xin", xs_), rr], writes=[("hn", hs_)])
                hslots.append(hs_)
            return hslots

        def in1(x_src_rows, ntile):
            return in1_compute(in1_dma(x_src_rows, ntile))

        def tr_stage(hslots, w, ntile):
            banks = [newbank(), newbank()]
            for t, hs_ in enumerate(hslots):
                for k in range(KT):
                    b = banks[k // 4]
                    pg.op("pe", lambda e, b=b, k=k, t=t, hs_=hs_: e.transpose(
                        out=ps_bf(b)[:, (k % 4) * STK + t * P:(k % 4) * STK + (t + 1) * P],
                        in_=hn[hs_][:, k * P:(k + 1) * P], identity=ident_bf[:, :]),
                        reads=[("hn", hs_), "ident_bf"], writes=[psr(b)])
            n = ntile * P
            for k in range(KT):
                b = banks[k // 4]
                pg.op("dve", lambda e, b=b, k=k, n=n: e.tensor_scalar(
                    out=hT[:, k, 0:n], in0=ps_bf(b)[:, (k % 4) * STK:(k % 4) * STK + n],
                    scalar1=A2[:, k, w:w + 1], scalar2=mod[:, k, w:w + 1],
                    op0=ALU.mult, op1=ALU.add),
                    reads=[psr(b), "A2", ("mod", 0)], writes=[("hT", k)])

        def in_stage(x_src_rows, w, ntile):
            tr_stage(in1(x_src_rows, ntile), w, ntile)

        def t_matmul(blk, t):
            b = newbank()
            for k in range(KT):
                pg.op("pe", lambda e, b=b, k=k, t=t, blk=blk: e.matmul(
                    ps[b][:, :], lhsT=hT[:, k, t * P:(t + 1) * P],
                    rhs=w_in_bf[:, k, blk * 512:(blk + 1) * 512],
                    start=(k == 0), stop=(k == KT - 1)),
                    reads=[("hT", k), ("win", blk)], writes=[psr(b)])
            return b

        def f_matmul(blk, j0, n):
            b = newbank()
            for jj in range(2):
                c0 = blk * 512 + (j0 + jj) * P
                for k in range(KT):
                    pg.op("pe", lambda e, b=b, k=k, jj=jj, c0=c0, n=n: e.matmul(
                        ps[b][:, jj * STK:jj * STK + n], lhsT=w_in_bf[:, k, c0:c0 + P],
                        rhs=hT[:, k, 0:n], start=(k == 0), stop=(k == KT - 1)),
                        reads=[("hT", k), ("win", blk)], writes=[psr(b)])
            return b

        def bank3(b, n):
            return ps[b][:, :].rearrange("p (j t) -> p j t", t=STK)[:, :, 0:n]

        def v_evac(b, vslot, vbuf, vres, mask_ap, mask_res):
            src = ps[b][:, :].rearrange("p (h d) -> p h d", d=64)
            if mask_ap is None:
                pg.op("dve", lambda e: e.tensor_copy(out=vbuf[:, vslot, :, 0:64], in_=src),
                      reads=[psr(b)], writes=[(vres, vslot)])
                pg.op("pool", lambda e: e.memset(vbuf[:, vslot, :, 64:65], 1.0),
                      reads=[], writes=[(vres, vslot, "one")])
            else:
                pg.op("dve", lambda e: e.tensor_scalar(
                    out=vbuf[:, vslot, :, 0:64], in0=src, scalar1=mask_ap, scalar2=None,
                    op0=ALU.mult), reads=[psr(b), mask_res], writes=[(vres, vslot)])
                pg.op("dve", lambda e: e.tensor_copy(
                    out=vbuf[:, vslot, :, 64:65],
                    in_=mask_ap.unsqueeze(2).to_broadcast([P, 8, 1])),
                    reads=[mask_res], writes=[(vres, vslot, "one")])

        def att_make(t, keysrc):
            order = [1, 2, 0, 3]
            groups = []
            ybs = nxt("ybt", 2)

            def qk_group(g):
                pts = nxt("pt", NPT)
                bx = [newbank(), newbank()]
                by = newbank()
                for sl, kt in enumerate(order + [4]):
                    kap, kres, _, _ = keysrc[kt]
                    if kt == 4:
                        outap = ps[by][:, 0:2 * P]
                        wr = psr(by)
                    else:
                        outap = ps[bx[sl // 2]][:, (sl % 2) * 2 * P:(sl % 2 + 1) * 2 * P]
                        wr = psr(bx[sl // 2])
                    pg.op("pe", lambda e, outap=outap, kap=kap, g=g: e.matmul(
                        outap, lhsT=kap(g), rhs=qT2[:, g, :, t * P:(t + 1) * P],
                        start=True, stop=True),
                        reads=[kres(g), ("qT", g)], writes=[wr])
                for xb in range(2):
                    pg.op("act", lambda e, xb=xb, pts=pts, bx=bx: e.activation(
                        out=PT[pts][:, :, 2 * xb:2 * xb + 2, :].rearrange("p h s q -> p s h q"),
                        in_=ps[bx[xb]][:, :].rearrange("p (s h q) -> p s h q", h=2, q=P),
                        func=AF.Exp),
                        reads=[psr(bx[xb])], writes=[("PT", pts, xb)])
                pg.op("act", lambda e, pts=pts, by=by: e.activation(
                    out=PT[pts][:, :, 4, :],
                    in_=ps[by][:, 0:2 * P].rearrange("p (h q) -> p h q", q=P), func=AF.Exp),
                    reads=[psr(by)], writes=[("PT", pts, 2)])
                pg.op("pool", lambda e, pts=pts: e.memset(PT[pts][0:64, :, 2, 64:128], 0.0),
                      reads=[], writes=[("PT", pts, 1)])
                pg.op("dve", lambda e, pts=pts, g=g: e.tensor_tensor(
                    out=PT[pts][:, :, 3:5, :], in0=PT[pts][:, :, 3:5, :],
                    in1=E[:, 2 * g:2 * g + 2, 1:3, :], op=ALU.mult),
                    reads=[("PT", pts, 1), ("PT", pts, 2), ("E", 1), ("E", 2)],
                    writes=[("PT", pts, 1), ("PT", pts, 2)])
                return pts

            pv_state = {"bank": None}

            def pv_group(g, pts):
                if g % 2 == 0:
                    pv_state["bank"] = newbank()
                b = pv_state["bank"]
                for hh in range(2):
                    h = 2 * g + hh
                    col = (h % 4) * 65
                    for i, kt in enumerate(order + [4]):
                        _, _, vap, vres = keysrc[kt]
                        pg.op("pe", lambda e, b=b, col=col, pts=pts, hh=hh, i=i, vap=vap, h=h:
                              e.matmul(ps[b][:, col:col + 65], lhsT=PT[pts][:, hh, i, :],
                                       rhs=vap(h), start=(i == 0), stop=(i == 4)),
                              reads=[("PT", pts, 0), ("PT", pts, 1), ("PT", pts, 2)] + list(vres),
                              writes=[psr(b)])
                if g % 2 == 1:
                    hg = g // 2
                    r = nxt("rd", 2)
                    pv3 = ps[b][:, 0:260].rearrange("p (h d) -> p h d", d=65)
                    pg.op("dve", lambda e, r=r, pv3=pv3: e.reciprocal(
                        out=rdt[r][:, :], in_=pv3[:, :, 64]),
                        reads=[psr(b)], writes=[("rd", r)])
                    pg.op("dve", lambda e, r=r: e.tensor_scalar(
                        out=rdt[r][:, :], in0=rdt[r][:, :], scalar1=0.5, scalar2=None,
                        op0=ALU.mult), reads=[("rd", r)], writes=[("rd", r)])
                    pg.op("dve", lambda e, r=r, pv3=pv3, hg=hg: e.tensor_tensor(
                        out=yBt[ybs][:, hg * 256:(hg + 1) * 256].rearrange("p (h d) -> p h d", d=64),
                        in0=pv3[:, :, 0:64],
                        in1=rdt[r][:, :].unsqueeze(2).to_broadcast([P, 4, 64]), op=ALU.mult),
                        reads=[psr(b), ("rd", r)], writes=[("yBt", ybs, hg)])

            def yb_finish():
                b = newbank()
                for j in range(4):
                    pg.op("pe", lambda e, b=b, j=j: e.transpose(
                        out=ps_bf(b)[:, j * P:(j + 1) * P], in_=yBt[ybs][:, j * P:(j + 1) * P],
                        identity=ident_bf[:, :]),
                        reads=[("yBt", ybs, j // 2), "ident_bf"], writes=[psr(b)])
                pg.op("dve", lambda e, b=b: e.tensor_tensor(
                    out=o_inT[:, 4:8, t * P:(t + 1) * P],
                    in0=ps_bf(b)[:, 0:512].rearrange("p (j q) -> p j q", q=P),
                    in1=sgBT[:, :, t * P:(t + 1) * P], op=ALU.mult),
                    reads=[psr(b), "sgBT"], writes=[("o_inT", 1, t)])
            return qk_group, pv_group, yb_finish

        def attention_pair(t, keysrc, n_q):
            qk_group, pv_group, yb_finish = att_make(t, keysrc)
            pending = []
            for g in range(4):
                pts = qk_group(g)
                pending.append((g, pts))
                if len(pending) > 1:
                    pv_group(*pending.pop(0))
            while pending:
                pv_group(*pending.pop(0))
            yb_finish()

        def gmlp_tile(t, vs_):
            b = newbank()
            for g in range(8):
                pg.op("pe", lambda e, b=b, g=g: e.matmul(
                    ps[b][:, g * 64:(g + 1) * 64], lhsT=wsT[:, g, :],
                    rhs=vAn[vs_][:, g * 64:(g + 1) * 64], start=True, stop=True),
                    reads=["wsT", ("vAn", vs_)], writes=[psr(b)])
            ms_ = nxt("mix", 2)
            pg.op("dve", lambda e, b=b, ms_=ms_: e.tensor_tensor(
                out=mixb[ms_][:, :], in0=ps[b][:, :], in1=LGh[:, :], op=ALU.mult),
                reads=[psr(b), "LGh"], writes=[("mixb", ms_)])
            return ms_

        def gmlp_tile_finish(t, ms_):
            b = newbank()
            for j in range(4):
                pg.op("pe", lambda e, b=b, j=j: e.transpose(
                    out=ps_bf(b)[:, j * P:(j + 1) * P], in_=mixb[ms_][:, j * P:(j + 1) * P],
                    identity=ident_bf[:, :]), reads=[("mixb", ms_), "ident_bf"], writes=[psr(b)])
            pg.op("dve", lambda e, b=b: e.tensor_tensor(
                out=tmpA[:, :].rearrange("p (j t) -> p j t", t=P),
                in0=ps_bf(b)[:, 0:512].rearrange("p (j t) -> p j t", t=P),
                in1=Ch[:, :, :], op=ALU.add), reads=[psr(b), "Ch"], writes=["tmpA"])
            pg.op("pool", lambda e: e.tensor_tensor(
                out=o_inT[:, 0:4, t * P:(t + 1) * P],
                in0=tmpA[:, :].rearrange("p (j t) -> p j t", t=P),
                in1=guT[:, :, t * P:(t + 1) * P], op=ALU.mult),
                reads=["tmpA", "guT"], writes=[("o_inT", 0, t)])

        def xres_load(x_rows):
            xr = nxt("xres", NXRES)
            dma("sp", xres[xr][:, :], x_rows, [], [("xres", xr)], f"xres{xr}")
            return xr

        def out_mm(t):
            banks = []
            for hf in range(2):
                b = newbank()
                banks.append(b)
                for k in range(KT):
                    pg.op("pe", lambda e, b=b, k=k, hf=hf: e.matmul(
                        ps[b][:, :], lhsT=o_inT[:, k, t * P:(t + 1) * P],
                        rhs=w_out_bf[:, k, hf * 512:(hf + 1) * 512],
                        start=(k == 0), stop=(k == KT - 1)),
                        reads=[("o_inT", k // 4, t), ("wout", hf)], writes=[psr(b)])
            return banks

        def out_epi(banks, xr, y_rows, ggt=None, ggres="GG"):
            ggt = GG if ggt is None else ggt
            s4 = nxt("st4", NST4)
            for hf in range(2):
                b = banks[hf]
                pg.op("act", lambda e, b=b, hf=hf, s4=s4: e.activation(
                    out=tmpo[:, :], in_=ps[b][:, :], func=AF.Square,
                    accum_out=st4[:, s4, 2 * hf:2 * hf + 1] if hf == 0 else st4[:, s4, 3:4]),
                    reads=[psr(b)], writes=["tmpo", (("st4", s4), "s%d" % hf)])
            pg.op("dve", lambda e, s4=s4: e.tensor_tensor(
                out=st4[:, s4, 0:1], in0=st4[:, s4, 0:1], in1=st4[:, s4, 3:4], op=ALU.add),
                reads=[(("st4", s4), "s0"), (("st4", s4), "s1")], writes=[(("st4", s4), 0)])
            rr = rsqrt_chain(st4[:, s4, 0:1], (("st4", s4), 0), 1.0 / D, s4)
            for hf in range(2):
                b = banks[hf]
                pg.op("dve", lambda e, b=b, hf=hf, s4=s4: e.scalar_tensor_tensor(
                    out=tmpo[:, :], in0=ps[b][:, :], scalar=st4[:, s4, 2:3],
                    in1=ggt[:, hf * 512:(hf + 1) * 512], op0=ALU.mult, op1=ALU.mult),
                    reads=[psr(b), rr, (ggres, hf)], writes=["tmpo"])
                pg.op("pool", lambda e, hf=hf, xr=xr: e.tensor_tensor(
                    out=xres[xr][:, hf * 512:(hf + 1) * 512], in0=xres[xr][:, hf * 512:(hf + 1) * 512],
                    in1=tmpo[:, :], op=ALU.add),
                    reads=[("xres", xr), "tmpo"], writes=[("xres", xr)])
            dma("sp", y_rows, xres[xr][:, :], [("xres", xr)], [], f"xres{xr}")

        def out_tile(t, w, xr, y_rows):
            if w == 1:
                out_epi(out_mm(t), xr, y_rows, GGs, "GGs")
            else:
                out_epi(out_mm(t), xr, y_rows)

        def st_xrows(kind, s):
            if kind == "halo":
                base = (s + NHALO) * STK
                return lambda t: xh[base + t * P: base + (t + 1) * P, :]
            elif kind == "main":
                base = s * STK
                return lambda t: xq[base + t * P: base + (t + 1) * P, :]
            return lambda t: xs_d[:, :]

        def st_proj(kind, s):
            w = 1 if kind == "sample" else 0
            ntile = 1 if kind == "sample" else NTL
            n = ntile * P
            last = (kind == "main" and (s + 1) * STK > NTOK - LASTN)

            van_slots = []
            for t in range(ntile):
                gt = 2 * s + t
                vslot = (gt + 2 * NHALO) % RK
                if kind != "halo":
                    b = t_matmul(BLK_VA, t)
                    gs_ = nxt("gv", 2)
                    vs_ = nxt("van", 2)
                    bs_ = nxt("bn", 2)
                    s4 = nxt("st4", NST4)
                    pg.op("act", lambda e, b=b, gs_=gs_: e.activation(
                        out=gv[gs_][:, :], in_=ps[b][:, :], func=AF.Gelu_apprx_tanh),
                        reads=[psr(b)], writes=[("gv", gs_)])
                    pg.op("dve", lambda e, gs_=gs_, bs_=bs_: e.bn_stats(
                        out=bnst[bs_][:, :], in_=gv[gs_][:, :]),
                        reads=[("gv", gs_)], writes=[("bnst", bs_)])
                    pg.op("dve", lambda e, bs_=bs_: e.bn_aggr(out=bnmv[bs_][:, :], in_=bnst[bs_][:, :]),
                          reads=[("bnst", bs_)], writes=[("bnmv", bs_)])
                    rr = rsqrt_chain(bnmv[bs_][:, 1:2], ("bnmv", bs_), 1.0, s4)
                    pg.op("dve", lambda e, bs_=bs_, s4=s4: e.tensor_scalar(
                        out=st4[:, s4, 3:4], in0=bnmv[bs_][:, 0:1], scalar1=-1.0,
                        scalar2=st4[:, s4, 2:3], op0=ALU.mult, op1=ALU.mult),
                        reads=[("bnmv", bs_), rr], writes=[(("st4", s4), 3)])
                    if kind == "sample":
                        pg.op("act", lambda e, gs_=gs_, s4=s4: e.activation(
                            out=vAn_f, in_=gv[gs_][:, :], func=AF.Identity,
                            bias=st4[:, s4, 3:4], scale=st4[:, s4, 2:3]),
                            reads=[("gv", gs_), rr, (("st4", s4), 3)], writes=["vAn_f"])
                        pg.op("dve", lambda e, vs_=vs_: e.tensor_copy(out=vAn[vs_][:, :], in_=vAn_f),
                              reads=["vAn_f"], writes=[("vAn", vs_)])
                        pg.op("dve", lambda e: e.scalar_tensor_tensor(
                            out=gmv_s, in0=vAn_f, scalar=2.0, in1=LGh[:, :],
                            op0=ALU.mult, op1=ALU.mult), reads=["vAn_f", "LGh"], writes=["gmv_s"])
                        pg.op("pool", lambda e: e.tensor_tensor(
                            out=gmv_s, in0=gmv_s, in1=LBf[:, :], op=ALU.add),
                            reads=["gmv_s", "LBf"], writes=["gmv_s"])
                        dma("sp", gmv_d[:, :], gmv_s, ["gmv_s"], [], "o_gmv")
                    else:
                        pg.op("act", lambda e, gs_=gs_, vs_=vs_, s4=s4: e.activation(
                            out=vAn[vs_][:, :], in_=gv[gs_][:, :], func=AF.Identity,
                            bias=st4[:, s4, 3:4], scale=st4[:, s4, 2:3]),
                            reads=[("gv", gs_), rr, (("st4", s4), 3)], writes=[("vAn", vs_)])
                    van_slots.append(vs_)
                b = t_matmul(BLK_V, t)
                if kind == "halo":
                    v_evac(b, vslot, V, "V", hval[:, 0:1], "hval")
                elif kind == "main":
                    v_evac(b, vslot, V, "V", None, None)
                    if last:
                        r0 = s * STK + t * P - (NTOK - LASTN)
                        pg.op("act", lambda e, b=b: e.activation(out=nvst, in_=ps[b][:, :],
                                                                 func=AF.Identity),
                              reads=[psr(b)], writes=["nvst"])
                        dma("sp", nv_d[r0:r0 + P, :], nvst, ["nvst"], [], "o_nv")
                else:
                    v_evac(b, 4, Vs, "Vs", rmask[:, 0:1], "rmask")
                    pg.op("act", lambda e, b=b: e.activation(out=nvst, in_=ps[b][:, :],
                                                             func=AF.Identity),
                          reads=[psr(b)], writes=["nvst"])
                    dma("sp", nvs_d[:, :], nvst, ["nvst"], [], "o_nv")
                if last or kind == "sample":
                    b = t_matmul(BLK_K, t)
                    pg.op("act", lambda e, b=b: e.activation(out=nkst, in_=ps[b][:, :],
                                                             func=AF.Identity),
                          reads=[psr(b)], writes=["nkst"])
                    if kind == "sample":
                        dma("sp", nks_d[:, :], nkst, ["nkst"], [], "o_nk")
                    else:
                        r0 = s * STK + t * P - (NTOK - LASTN)
                        dma("sp", nk_d[r0:r0 + P, :], nkst, ["nkst"], [], "o_nk")

            for j0 in (0, 2):
                b = f_matmul(BLK_K, j0, n)
                if kind == "sample":
                    pg.op("dve", lambda e, b=b, j0=j0: e.tensor_copy(
                        out=kTs[:, j0:j0 + 2, 4 * P:5 * P], in_=bank3(b, n)),
                        reads=[psr(b)], writes=[("kTs", 4, j0), ("kTs", 4, j0 + 1)])
                else:
                    pos = ((2 * s + 2 * NHALO) % RK) * P
                    pg.op("dve", lambda e, b=b, j0=j0, pos=pos: e.tensor_copy(
                        out=kT[:, j0:j0 + 2, pos:pos + STK], in_=bank3(b, n)),
                        reads=[psr(b)],
                        writes=[("kT", (2 * s + tt + 2 * NHALO) % RK, j0 + jj)
                                for tt in range(2) for jj in range(2)])
            if kind == "halo":
                return van_slots
            for j0 in (0, 2):
                b = f_matmul(BLK_Q, j0, n)
                pg.op("dve", lambda e, b=b, j0=j0: e.tensor_scalar(
                    out=qT2[0:64, j0:j0 + 2, 0, 0:n], in0=bank3(b, n)[0:64], scalar1=0.125,
                    scalar2=None, op0=ALU.mult), reads=[psr(b)],
                    writes=[("qT", j0), ("qT", j0 + 1)])
                pg.op("dve", lambda e, b=b, j0=j0: e.tensor_scalar(
                    out=qT2[64:128, j0:j0 + 2, 1, 0:n], in0=bank3(b, n)[64:128], scalar1=0.125,
                    scalar2=None, op0=ALU.mult), reads=[psr(b)],
                    writes=[("qT", j0), ("qT", j0 + 1)])
            for j0 in (0, 2):
                b = f_matmul(BLK_U, j0, n)
                pg.op("act", lambda e, b=b, j0=j0: e.activation(
                    out=guT[:, j0:j0 + 2, 0:n], in_=bank3(b, n), func=AF.Gelu_apprx_tanh),
                    reads=[psr(b)], writes=["guT"])
            if kind == "main":
                pg.op("act", lambda e: e.activation(out=dout_[:, 0:1], in_=dzero[:, :], func=AF.Exp),
                      reads=["dzero"], writes=["dout0"])
            for blk, dst, dres in ((BLK_GA, sgAT, "sgAT"), (BLK_GB, sgBT, "sgBT")):
                for j0 in (0, 2):
                    b = f_matmul(blk, j0, n)
                    th = nxt("tht", 2)
                    thv = tht[th][:, :].rearrange("p (j t) -> p j t", t=STK)[:, :, 0:n]
                    pg.op("act", lambda e, b=b, thv=thv: e.activation(
                        out=thv, in_=bank3(b, n), func=AF.Tanh, scale=0.5),
                        reads=[psr(b)], writes=[("tht", th)])
                    pg.op("dve", lambda e, b=b, thv=thv, dst=dst, j0=j0: e.scalar_tensor_tensor(
                        out=dst[:, j0:j0 + 2, 0:n], in0=thv, scalar=1.0, in1=bank3(b, n),
                        op0=ALU.add, op1=ALU.mult),
                        reads=[psr(b), ("tht", th)], writes=[dres])
            pg.op("pool", lambda e: e.tensor_tensor(
                out=guT[:, :, 0:n], in0=guT[:, :, 0:n], in1=sgAT[:, :, 0:n], op=ALU.mult),
                reads=["guT", "sgAT"], writes=["guT"])

            return van_slots

        def st_keysrc(kind, s, t):
            keysrc = []
            if kind == "main":
                gt = 2 * s + t
                for kt in range(5):
                    sl = (gt - 4 + kt + 2 * NHALO) % RK
                    keysrc.append((
                        (lambda hp, sl=sl: kT[:, hp, sl * P:(sl + 1) * P]),
                        (lambda hp, sl=sl: ("kT", sl, hp)),
                        (lambda h, sl=sl: V[:, sl, h, :]),
                        (("V", sl), ("V", sl, "one"))))
            else:
                for kt in range(5):
                    vres = (("Vs", kt), ("Vs", kt, "one")) if kt == 4 else (("Vs", kt),)
                    keysrc.append((
                        (lambda hp, kt=kt: kTs[:, hp, kt * P:(kt + 1) * P]),
                        (lambda hp, kt=kt: ("kTs", kt, hp)),
                        (lambda h, kt=kt: Vs[:, kt, h, :]),
                        vres))
            return keysrc

        def gate_setup():
            bg = newbank()
            wada_mm(4, gate_slots[0], bg)
            wada_mm(5, gate_slots[1], bg)
            pg.op("dve", lambda e: e.tensor_tensor(
                out=mod[:, 16:24, :], in0=ps[bg][:, 32:48].rearrange("p (j w) -> p j w", w=2),
                in1=badac[:, 16:24].unsqueeze(2).to_broadcast([P, 8, 2]), op=ALU.add),
                reads=[psr(bg), "badac"], writes=[("mod", 1)])
            pg.op("dve", lambda e: e.tensor_tensor(
                out=GC[:, :, :], in0=mod[:, 16:24, :],
                in1=gpostc[:, :].unsqueeze(2).to_broadcast([P, 8, 2]), op=ALU.mult),
                reads=[("mod", 1), "gpostc"], writes=["GC"])
            make_GG(0)

        for s in range(-NHALO, 0):
            in_stage(st_xrows("halo", s), 0, NTL)
            st_proj("halo", s)

        def sample_prep():
            alias = [("wada", 0), ("wada", 1)]
            dma("pool", ck_bf, ck_d.rearrange("(t p) c -> p t c", p=P), [], ["ck_bf"] + alias, "c_ck")
            for i4 in range(4):
                dma("pool", Vs[:, i4, :, 0:64],
                    cv_d[i4 * P:(i4 + 1) * P, :].rearrange("p (h d) -> p h d", d=64), [],
                    [("Vs", i4)] + alias, f"c_cv{i4}")
            pg.op("pool", lambda e: e.memset(Vs[:, 0:4, :, 64:65], 1.0), reads=[],
                  writes=[("Vs", i) for i in range(4)])

        def sample_prep_pe():
            alias = [("wada", 0), ("wada", 1)]
            for hp in range(4):
                b = newbank()
                for kt in range(4):
                    pg.op("pe", lambda e, b=b, kt=kt, hp=hp: e.transpose(
                        out=ps_bf(b)[:, kt * P:(kt + 1) * P], in_=ck_bf[:, kt, hp * P:(hp + 1) * P],
                        identity=ident_bf[:, :]), reads=["ck_bf", "ident_bf"], writes=[psr(b)])
                pg.op("dve", lambda e, b=b, hp=hp: e.tensor_copy(
                    out=kTs[:, hp, 0:4 * P], in_=ps_bf(b)[:, 0:4 * P]),
                    reads=[psr(b)], writes=[("kTs", kt_, hp) for kt_ in range(4)] + alias)

        LAG = 3
        hs_cur = in1(st_xrows("main", 0), NTL)
        tr_stage(hs_cur, 0, NTL)
        for s in range(NST):
            xr = [xres_load(xq[s * STK + t * P: s * STK + (t + 1) * P, :]) for t in range(NTL)]
            if s + 1 < NST:
                xs_next = in1_dma(st_xrows("main", s + 1), NTL)
            elif sample:
                xs_next = in1_dma(st_xrows("sample", 0), 1)
            else:
                xs_next = None
            van_slots = st_proj("main", s)
            if s == 0:
                gate_setup()
            if s == min(2, NST - 1) and sample:
                sample_prep()
            if s == min(6, NST - 1) and sample:
                sample_prep_pe()
            if s == min(8, NST - 1) and sample:
                make_GG(1, GGs, "GGs")
            mix_slots = [gmlp_tile(t, van_slots[t]) for t in range(NTL)]
            atts = [att_make(t, st_keysrc("main", s, t)) for t in range(NTL)]
            stream = [(t, g) for t in range(NTL) for g in range(4)]
            pend = []
            hs_next = None
            for i, (t, g) in enumerate(stream):
                pts = atts[t][0](g)
                pend.append((t, g, pts))
                if i == 1:
                    for t2 in range(NTL):
                        gmlp_tile_finish(t2, mix_slots[t2])
                if len(pend) > LAG:
                    t3, g3, p3 = pend.pop(0)
                    atts[t3][1](g3, p3)
            nfl = 0
            while pend:
                t3, g3, p3 = pend.pop(0)
                atts[t3][1](g3, p3)
                nfl += 1
                if nfl == 1:
                    atts[0][2]()
            pg.op("act", lambda e: e.activation(out=dout_[:, 1:2], in_=dzero[:, :],
                                                func=AF.Gelu_apprx_tanh),
                  reads=["dzero"], writes=["dout1"])
            if xs_next is not None:
                hs_next = in1_compute(xs_next)
            bk0 = out_mm(0)
            atts[1][2]()
            out_epi(bk0, xr[0], y_d[s * STK: s * STK + P, :])
            if s + 1 < NST:
                tr_stage(hs_next, 0, NTL)
            elif sample:
                tr_stage(hs_next, 1, 1)
            out_tile(1, 0, xr[1], y_d[s * STK + P: s * STK + 2 * P, :])

        def proc_sample():
            xr = xres_load(xs_d[:, :])
            van_slots = st_proj("sample", 0)
            ms_ = gmlp_tile(0, van_slots[0])
            gmlp_tile_finish(0, ms_)
            attention_pair(0, st_keysrc("sample", 0, 0), P)
            out_tile(0, 1, xr, ys_d[:, :])

        if sample:
            proc_sample()

        pg.finalize()
        eng_sems = {e: es.enter_context(nc.semaphore(f"sem_{e}")) for e in ("pe", "act", "dve", "pool")}
        chan_sems = {c: es.enter_context(nc.semaphore(f"ch_{c}")) for c in pg.chan_count}
        final = [(chan_sems[c], v) for c, v in pg.chan_count.items()]
        block = es.enter_context(nc.Block())

        @block.sync
        def _(e):
            pg.emit("sp", e, eng_sems, chan_sems, final_waits=final)

        @block.gpsimd
        def _(e):
            pg.emit("pool", e, eng_sems, chan_sems)

        @block.scalar
        def _(e):
            pg.emit("act", e, eng_sems, chan_sems)

        @block.vector
        def _(e):
            pg.emit("dve", e, eng_sems, chan_sems)

        @block.tensor
        def _(e):
            pg.emit("pe", e, eng_sems, chan_sems)
    return nc


def host_inputs(inp, NST=16, NHALO=2, n_cores=8):
    f = lambda a: np.ascontiguousarray(np.asarray(a, dtype=np.float32))
    xp = f(inp["x_prompt"])
    B, S, _ = xp.shape
    qn = NST * STK
    nq = S // qn
    xsm = f(inp["x_sample"])
    ck = f(inp["cache_attn_k"])[0]
    cv = f(inp["cache_attn_v"])[0]
    cpr = f(inp["c_prompt"])
    csm = f(inp["c_sample"])
    col = lambda v, n: np.ascontiguousarray(v.reshape(n, P).T)
    rb = f(inp["rel_bias"])[0]
    shared = {
        "w_ada": f(inp["w_ada"])[0],
        "b_ada_c": col(f(inp["b_ada"])[0], 24),
        "w_in": f(inp["w_in"])[0],
        "g_pre_c": col(f(inp["g_pre"])[0], 8),
        "g_post_c": col(f(inp["g_post"])[0], 8),
        "ln_g": f(inp["ln_g"])[0].reshape(1, 512),
        "ln_b": f(inp["ln_b"])[0].reshape(1, 512),
        "w_s": f(inp["w_s"])[0],
        "b_sT": np.ascontiguousarray(f(inp["b_s"])[0].T),
        "rbp": np.ascontiguousarray(np.pad(rb, ((0, 0), (0, 127)), mode="edge")),
        "rb_last": np.ascontiguousarray(rb[:, 256].reshape(1, 8)),
        "w_out": f(inp["w_out"])[0],
        "ident": np.eye(P, dtype=np.float32),
        "antiident": np.ascontiguousarray(np.eye(P, dtype=np.float32)[::-1]),
        "tril": np.tril(np.ones((P, P), dtype=np.float32)),
    }
    rmask = np.zeros((P, 1), np.float32)
    rmask[:xsm.shape[1]] = 1.0
    maps = []
    for c in range(n_cores):
        b, qd = c // nq, c % nq
        m = dict(shared)
        m["xq"] = np.ascontiguousarray(xp[b, qd * qn:(qd + 1) * qn])
        hal = np.zeros((NHALO * STK, D), np.float32)
        if qd > 0:
            hal[:] = xp[b, qd * qn - NHALO * STK: qd * qn]
        m["xh"] = hal
        m["hval"] = np.full((P, 1), 1.0 if qd > 0 else 0.0, np.float32)
        m["rmask"] = rmask
        xs = np.zeros((P, D), np.float32)
        xs[:xsm.shape[1]] = xsm[c]
        m["xs"] = xs
        m["ck"] = np.ascontiguousarray(ck[c].reshape(512, 512))
        m["cv"] = np.ascontiguousarray(cv[c].reshape(512, 512))
        cvv = np.zeros((P, 8, 2), np.float32)
        cvv[:, :, 0] = cpr[b].reshape(8, P).T
        cvv[:, :, 1] = csm[c].reshape(8, P).T
        m["cvec"] = np.ascontiguousarray(cvv.reshape(P, 16))
        maps.append(m)
    return maps


def assemble(results, B, S, NST=16, n_cores=8, dec_seq=16):
    qn = NST * STK
    nq = S // qn
    lastn = min(512, qn)
    yp = np.zeros((B, S, D), np.float32)
    ysm = np.zeros((n_cores, dec_seq, D), np.float32)
    nkp = np.zeros((1, B, lastn, 8, 64), np.float32)
    nvp = np.zeros((1, B, lastn, 8, 64), np.float32)
    nks = np.zeros((1, n_cores, dec_seq, 8, 64), np.float32)
    nvs = np.zeros((1, n_cores, dec_seq, 8, 64), np.float32)
    gm = np.zeros((1, n_cores, dec_seq, 512), np.float32)
    for c, r in enumerate(results):
        b, qd = c // nq, c % nq
        yp[b, qd * qn:(qd + 1) * qn] = r["y"]
        ysm[c] = r["ys"][:dec_seq]
        if qd == nq - 1:
            nkp[0, b] = r["nk"].reshape(lastn, 8, 64)
            nvp[0, b] = r["nv"].reshape(lastn, 8, 64)
        nks[0, c] = r["nks"][:dec_seq].reshape(dec_seq, 8, 64)
        nvs[0, c] = r["nvs"][:dec_seq].reshape(dec_seq, 8, 64)
        gm[0, c] = r["gmv"][:dec_seq]
    return (yp, ysm, nkp, nvp, nks, nvs, gm)


def kernel(**inputs):
    NST = 16
    maps = host_inputs(inputs, NST=NST)
    nc = build(NST=NST)
    res = run_bass_kernel_spmd(nc, maps, core_ids=list(range(8)))
    B, S, _ = np.asarray(inputs["x_prompt"]).shape
    return assemble(res.results, B, S, NST=NST)
```

```python
import numpy as np
import contextlib
import concourse.bass as bass
import concourse.mybir as mybir
from concourse.bass_utils import run_bass_kernel_spmd

F32 = mybir.dt.float32
BF16 = mybir.dt.bfloat16
AF = mybir.ActivationFunctionType
ALU = mybir.AluOpType

P = 128
D = 1024
KT = 8
DIN = 3584
STK = 256
NTL = 2
RK = 8
EPS = 1e-6
BLK_U, BLK_VA, BLK_GA, BLK_Q, BLK_K, BLK_V, BLK_GB = range(7)


class Op:
    __slots__ = ("eng", "fn", "deps", "signal", "count", "chan", "idx")


class Prog:
    ENGS = ("pe", "act", "dve", "pool", "sp")

    def __init__(self):
        self.ops = {e: [] for e in self.ENGS}
        self.lastw = {}
        self.readers = {}
        self.chan_count = {}
        self.chan_last = {}

    def op(self, eng, fn, reads=(), writes=(), chan=None):
        o = Op()
        o.eng, o.fn, o.chan, o.signal, o.count = eng, fn, chan, False, None
        o.idx = self.nops = getattr(self, "nops", 0) + 1
        deps = set()
        for r in reads:
            w = self.lastw.get(r)
            if w is not None:
                deps.add(w)
        for w_ in writes:
            w = self.lastw.get(w_)
            if w is not None:
                deps.add(w)
            for rd in self.readers.get(w_, ()):
                deps.add(rd)
        deps.discard(o)
        best = {}
        for d in deps:
            if d.eng == "pe" and eng == "pe" and d.chan is None and chan is None:
                continue
            key = ("c", d.chan) if d.chan is not None else ("e", d.eng)
            if key not in best or best[key].idx < d.idx:
                best[key] = d
        o.deps = list(best.values())
        for d in o.deps:
            d.signal = True
        for r in reads:
            self.readers.setdefault(r, []).append(o)
        for w_ in writes:
            self.lastw[w_] = o
            self.readers[w_] = []
        if chan is not None:
            c = self.chan_count.get(chan, 0) + 16
            self.chan_count[chan] = c
            o.count = c
            self.chan_last[chan] = o
        self.ops[eng].append(o)
        return o

    def finalize(self):
        for e in self.ENGS:
            c = 0
            for o in self.ops[e]:
                if o.chan is None and o.signal:
                    c += 1
                    o.count = c

    def emit(self, eng_name, engine, eng_sems, chan_sems, final_waits=()):
        waited = {}
        for o in self.ops[eng_name]:
            need = {}
            for d in o.deps:
                if d.chan is not None:
                    key = ("c", d.chan)
                    sem = chan_sems[d.chan]
                else:
                    key = ("e", d.eng)
                    sem = eng_sems[d.eng]
                v = d.count
                if key not in need or need[key][1] < v:
                    need[key] = (sem, v)
            for key, (sem, v) in need.items():
                if waited.get(key, 0) < v:
                    engine.wait_ge(sem, v)
                    waited[key] = v
            inst = o.fn(engine)
            if o.chan is not None:
                inst.then_inc(chan_sems[o.chan], 16)
            elif o.signal:
                inst.then_inc(eng_sems[eng_name], 1)
        for sem, v in final_waits:
            engine.wait_ge(sem, v)


def build(NST=16, NHALO=2, sample=True):
    nc = bass.Bass("TRN2", target_bir_lowering=False)
    NTOK = NST * STK
    NHT = NHALO * STK
    LASTN = min(512, NTOK)

    def din(name, shape, dt=F32):
        return nc.dram_tensor(name, list(shape), dt, kind="ExternalInput").ap()

    def dout(name, shape, dt=F32):
        return nc.dram_tensor(name, list(shape), dt, kind="ExternalOutput").ap()

    xq = din("xq", [NTOK, D])
    xh = din("xh", [NHT, D])
    hval_d = din("hval", [P, 1])
    rmask_d = din("rmask", [P, 1])
    xs_d = din("xs", [P, D])
    ck_d = din("ck", [512, 512])
    cv_d = din("cv", [512, 512])
    cvec_d = din("cvec", [P, 16])
    wada_d = din("w_ada", [D, 3 * D])
    badac_d = din("b_ada_c", [P, 24])
    win_d = din("w_in", [D, DIN])
    gprec_d = din("g_pre_c", [P, 8])
    gpostc_d = din("g_post_c", [P, 8])
    lng_d = din("ln_g", [1, 512])
    lnb_d = din("ln_b", [1, 512])
    ws_d = din("w_s", [8, P, P])
    bsT_d = din("b_sT", [P, 8])
    rbp_d = din("rbp", [8, 384])
    rbl_d = din("rb_last", [1, 8])
    wout_d = din("w_out", [D, D])
    ident_d = din("ident", [P, P])
    antid_d = din("antiident", [P, P])
    tril_d = din("tril", [P, P])

    y_d = dout("y", [NTOK, D])
    ys_d = dout("ys", [P, D])
    nk_d = dout("nk", [LASTN, 512])
    nv_d = dout("nv", [LASTN, 512])
    nks_d = dout("nks", [P, 512])
    nvs_d = dout("nvs", [P, 512])
    gmv_d = dout("gmv", [P, 512])

    pg = Prog()
    es = contextlib.ExitStack()

    def sb(name, shape, dt):
        return es.enter_context(nc.sbuf_tensor("s_" + name, list(shape), dt))

    with es:
        w_in_bf = sb("w_in_bf", [P, KT, DIN], BF16)
        w_out_bf = sb("w_out_bf", [P, KT, D], BF16)
        arena = sb("arena", [P, 15360], BF16)
        ident_bf = sb("ident_bf", [P, P], BF16)
        ident_f = sb("ident_f", [P, P], F32)
        ones_f = sb("ones_f", [P, P], F32)
        dgs = [sb(f"dg{i}", [P, P], F32) for i in range(2)]
        cvec = sb("cvec", [P, 8, 2], F32)
        thc = sb("thc", [P, 8, 2], F32)
        sc_bf = sb("sc_bf", [P, 8, 2], BF16)
        badac = sb("badac", [P, 24], F32)
        gprec = sb("gprec", [P, 8], F32)
        gpostc = sb("gpostc", [P, 8], F32)
        mod = sb("mod", [P, 24, 2], F32)
        A2 = sb("A2", [P, 8, 2], F32)
        GC = sb("GC", [P, 8, 2], F32)
        GG = sb("GG", [P, D], F32)
        LGh = sb("LGh", [P, 512], F32)
        LBf = sb("LBf", [P, 512], F32)
        bsT = sb("bsT", [P, 8], F32)
        wsT = sb("wsT", [P, 8, P], BF16)
        Ch = sb("Ch", [P, 4, P], F32)
        negc = sb("negc", [P, 8], F32)
        E = sb("E", [P, 8, 3, P], BF16)
        hval = sb("hval_s", [P, 1], F32)
        rmask = sb("rmask_s", [P, 1], F32)
        mhalf = sb("mhalf", [P, 1], F32)
        dzero = sb("dzero", [P, 1], F32)
        dout_ = sb("dout", [P, 2], F32)
        NXIN = 2
        xin = [sb(f"xin{i}", [P, D], F32) for i in range(NXIN)]
        hn = [sb(f"hn{i}", [P, D], BF16) for i in range(2)]
        NXRES = 2
        xres = [sb(f"xres{i}", [P, D], F32) for i in range(NXRES)]
        tmpo = sb("tmpo", [P, 512], F32)
        hT = sb("hT", [P, KT, STK], BF16)
        qT2 = sb("qT2", [P, 4, 2, STK], BF16)
        kT = sb("kT", [P, 4, RK * P], BF16)
        V = sb("V", [P, RK, 8, 65], BF16)
        guT = sb("guT", [P, 4, STK], BF16)
        sgAT = sb("sgAT", [P, 4, STK], BF16)
        sgBT = sb("sgBT", [P, 4, STK], BF16)
        tht = [sb(f"tht{i}", [P, 512], BF16) for i in range(2)]
        gv = [sb(f"gv{i}", [P, 512], F32) for i in range(2)]
        vAn = [sb(f"vAn{i}", [P, 512], BF16) for i in range(2)]
        mixb = [sb(f"mixb{i}", [P, 512], BF16) for i in range(2)]
        tmpA = sb("tmpA", [P, 512], BF16)
        NPT = 5
        PT = [sb(f"PT{i}", [P, 2, 5, P], BF16) for i in range(NPT)]
        yBt = [sb(f"yBt{i}", [P, 512], BF16) for i in range(2)]
        o_inT = sb("o_inT", [P, KT, STK], BF16)
        NST4 = 8
        st4 = sb("st4", [P, NST4, 4], F32)
        bnst = [sb(f"bnst{i}", [P, 6], F32) for i in range(2)]
        bnmv = [sb(f"bnmv{i}", [P, 2], F32) for i in range(2)]
        rdt = [sb(f"rd{i}", [P, 4], F32) for i in range(2)]
        ps = [es.enter_context(nc.psum_tensor(f"ps{i}", [P, 512], F32)) for i in range(8)]

        def aview(lo, hi, dt=BF16, shape=None):
            v = arena[:, lo:hi]
            if dt is F32:
                v = v.bitcast(F32)
            if shape is not None:
                names = " ".join(f"a{i}" for i in range(len(shape)))
                kw = {f"a{i}": s for i, s in enumerate(shape)}
                v = v.rearrange(f"p ({names}) -> p {names}", **kw)
            return v
        wada_bf = [aview(0, 4096, BF16, (8, 512)), aview(4096, 8192, BF16, (8, 512))]
        Hk = aview(8192, 12288, F32, (2, 8, P))
        ws_f = aview(12288, 14336, F32, (8, P))
        tril_f = aview(14336, 14592, F32)
        J_f = aview(14592, 14848, F32)
        LBb = aview(14848, 15360, BF16)
        kTs = aview(0, 2560, BF16, (4, 5 * P))
        Vs = aview(2560, 5160, BF16, (5, 8, 65))
        ck_bf = aview(5160, 7208, BF16, (4, 512))
        vAn_f = aview(7208, 8232, F32)
        gmv_s = aview(8232, 9256, F32)
        nkst = aview(9256, 10280, F32)
        nvst = aview(10280, 11304, F32)
        GGs = aview(11304, 13352, F32)

        psn = [0]

        def newbank():
            b = psn[0] % 8
            psn[0] += 1
            return b

        def psr(b):
            return ("ps", b)

        def ps_bf(b):
            return ps[b][:, :].bitcast(BF16)

        def dma(eng, out, in_, reads, writes, chan, **kw):
            return pg.op(eng, lambda e, out=out, in_=in_, kw=kw: e.dma_start(out=out, in_=in_, **kw),
                         reads=reads, writes=writes, chan=chan)

        def bcast_rows(src, n):
            return bass.AP(tensor=src.tensor, offset=src.offset, ap=[[0, P], [1, n]])

        dma("sp", cvec[:, :, :], cvec_d.rearrange("p (k w) -> p k w", w=2), [], ["cvec"], "c_cvec")
        dma("sp", badac[:, :], badac_d, [], ["badac"], "c_badac")
        dma("sp", gprec[:, :], gprec_d, [], ["gprec"], "c_gprec")
        dma("sp", gpostc[:, :], gpostc_d, [], ["gpostc"], "c_gpostc")
        dma("sp", hval[:, :], hval_d, [], ["hval"], "c_hval")
        dma("sp", rmask[:, :], rmask_d, [], ["rmask"], "c_rmask")
        dma("sp", ident_f[:, :], ident_d, [], ["ident_f"], "c_identf")
        dma("sp", J_f, antid_d, [], ["J_f"], "c_J")
        dma("sp", tril_f, tril_d, [], ["tril"], "c_tril")
        dma("sp", bsT[:, :], bsT_d, [], ["bsT"], "c_bsT")
        dma("sp", LGh[:, :], bcast_rows(lng_d, 512), [], ["LGh"], "c_lg")
        dma("sp", LBf[:, :], bcast_rows(lnb_d, 512), [], ["LBf"], "c_lb")
        dma("sp", negc[:, :], bcast_rows(rbl_d, 8), [], ["negc"], "c_negc")
        dma("sp", ws_f, ws_d.rearrange("g t s -> t g s"), [], ["ws_f"], "c_ws")
        for w_, off in ((0, 129), (1, 1)):
            src = bass.AP(tensor=rbp_d.tensor, offset=off, ap=[[1, P], [384, 8], [1, P]])
            dma("sp", Hk[:, w_, :, :], src, [], [("Hk", w_)], f"c_hk{w_}")
        dma("pool", ident_bf[:, :], ident_d, [], ["ident_bf"], "c_identb")

        pool_ms = lambda out, val, writes, reads=(): pg.op(
            "pool", lambda e, out=out, val=val: e.memset(out, val), reads=reads, writes=writes)
        pool_ms(ones_f[:, :], 1.0, ["ones_f"])
        pool_ms(mhalf[:, :], -0.5, ["mhalf"])
        pool_ms(dzero[:, :], 0.0, ["dzero"])
        pool_ms(qT2[:, :, :, :], 0.0, [("qT", i) for i in range(4)])
        pool_ms(E[:, :, 0, :], 1.0, ["E0"])
        pool_ms(E[0:64, :, 0, 64:128], 0.0, ["E0"])

        pg.op("act", lambda e: e.activation(out=thc[:, :, :], in_=cvec[:, :, :], func=AF.Tanh, scale=0.5),
              reads=["cvec"], writes=["thc"])
        pg.op("dve", lambda e: e.scalar_tensor_tensor(out=thc[:, :, :], in0=thc[:, :, :], scalar=1.0,
                                                      in1=cvec[:, :, :], op0=ALU.add, op1=ALU.mult),
              reads=["thc", "cvec"], writes=["thc"])
        pg.op("dve", lambda e: e.tensor_scalar(out=sc_bf[:, :, :], in0=thc[:, :, :], scalar1=0.5,
                                               scalar2=None, op0=ALU.mult),
              reads=["thc"], writes=["sc_bf"])

        wada_i = [0]

        def wada_dma(ci):
            slot = wada_i[0] % 2
            wada_i[0] += 1
            dma("pool", wada_bf[slot],
                wada_d[:, ci * 512:(ci + 1) * 512].rearrange("(k p) c -> p k c", p=P),
                [], [("wada", slot)], f"c_wada{slot}")
            return slot

        def wada_mm(ci, slot, bank):
            for jj in range(4):
                j = ci * 4 + jj
                for k in range(KT):
                    pg.op("pe", lambda e, j=j, jj=jj, k=k, slot=slot, bank=bank: e.matmul(
                        ps[bank][:, 2 * j:2 * j + 2], lhsT=wada_bf[slot][:, k, jj * P:(jj + 1) * P],
                        rhs=sc_bf[:, k, :], start=(k == 0), stop=(k == KT - 1)),
                        reads=[("wada", slot), "sc_bf"], writes=[psr(bank)])

        def mod_chunks(chunks, bank):
            for ci in chunks:
                wada_mm(ci, wada_dma(ci), bank)

        bmod = newbank()
        mod_chunks([2, 3, 0, 1], bmod)
        pg.op("dve", lambda e: e.tensor_tensor(
            out=mod[:, 0:16, :], in0=ps[bmod][:, 0:32].rearrange("p (j w) -> p j w", w=2),
            in1=badac[:, 0:16].unsqueeze(2).to_broadcast([P, 16, 2]), op=ALU.add),
            reads=[psr(bmod), "badac"], writes=[("mod", 0)])
        pg.op("dve", lambda e: e.scalar_tensor_tensor(
            out=A2[:, :, :], in0=mod[:, 8:16, :], scalar=1.0,
            in1=gprec[:, :].unsqueeze(2).to_broadcast([P, 8, 2]), op0=ALU.add, op1=ALU.mult),
            reads=[("mod", 0), "gprec"], writes=["A2"])

        for c0, c1, blks in ((2048, 3072, (BLK_K, BLK_V)), (0, 1024, (BLK_U, BLK_VA)),
                             (1024, 2048, (BLK_GA, BLK_Q)), (3072, 3584, (BLK_GB,))):
            dma("pool", w_in_bf[:, :, c0:c1],
                win_d[:, c0:c1].rearrange("(k p) c -> p k c", p=P),
                [], [("win", b_) for b_ in blks], f"c_win{c0}")
        gate_slots = [wada_dma(4), wada_dma(5)]
        dma("pool", w_out_bf[:, :, :], wout_d.rearrange("(k p) c -> p k c", p=P),
            [], [("wout", 0), ("wout", 1)], "c_wout")

        def make_GG(w, dst=None, dres="GG"):
            dst = GG if dst is None else dst
            for hf in range(2):
                b = newbank()
                for kk in range(4):
                    k = hf * 4 + kk
                    dg = dgs[k % 2]
                    pg.op("dve", lambda e, dg=dg, k=k: e.tensor_scalar(
                        out=dg[:, :], in0=ident_f[:, :], scalar1=GC[:, k, w:w + 1], scalar2=None,
                        op0=ALU.mult), reads=["ident_f", "GC"], writes=[("dg", k % 2)])
                    pg.op("pe", lambda e, b=b, kk=kk, dg=dg: e.matmul(
                        ps[b][:, kk * P:(kk + 1) * P], lhsT=ones_f[:, :], rhs=dg[:, :],
                        start=True, stop=True),
                        reads=["ones_f", ("dg", k % 2)], writes=[psr(b)])
                pg.op("dve", lambda e, b=b, hf=hf, dst=dst: e.tensor_copy(
                    out=dst[:, hf * 512:(hf + 1) * 512], in_=ps[b][:, :]),
                    reads=[psr(b)], writes=[(dres, hf)])
        pg.op("dve", lambda e: e.tensor_copy(out=LBb, in_=LBf[:, :]), reads=["LBf"], writes=["LBb"])
        pg.op("dve", lambda e: e.tensor_tensor(
            out=ws_f, in0=ws_f, in1=tril_f.unsqueeze(1).to_broadcast([P, 8, P]), op=ALU.mult),
            reads=["ws_f", "tril"], writes=["ws_f"])
        for hf in range(2):
            b = newbank()
            for gg in range(4):
                g = hf * 4 + gg
                pg.op("pe", lambda e, b=b, gg=gg, g=g: e.transpose(
                    out=ps[b][:, gg * P:(gg + 1) * P], in_=ws_f[:, g, :], identity=ident_f[:, :]),
                    reads=["ws_f", "ident_f"], writes=[psr(b)])
            pg.op("dve", lambda e, b=b, hf=hf: e.tensor_copy(
                out=wsT[:, hf * 4:(hf + 1) * 4, :],
                in_=ps[b][:, :].rearrange("p (g t) -> p g t", t=P)),
                reads=[psr(b)], writes=["wsT"])
        b = newbank()
        for g in range(8):
            pg.op("pe", lambda e, b=b, g=g: e.matmul(
                ps[b][:, g * 64:(g + 1) * 64], lhsT=wsT[:, g, :], rhs=LBb[:, g * 64:(g + 1) * 64],
                start=True, stop=True), reads=["wsT", "LBb"], writes=[psr(b)])
        pg.op("dve", lambda e, b=b: e.tensor_tensor(
            out=gv[0][:, :].rearrange("p (g d) -> p g d", d=64),
            in0=ps[b][:, :].rearrange("p (g d) -> p g d", d=64),
            in1=bsT[:, :].unsqueeze(2).to_broadcast([P, 8, 64]), op=ALU.add),
            reads=[psr(b), "bsT"], writes=[("gv", 0)])
        pg.op("dve", lambda e: e.tensor_scalar(out=gv[0][:, :], in0=gv[0][:, :], scalar1=0.5,
                                               scalar2=None, op0=ALU.mult),
              reads=[("gv", 0)], writes=[("gv", 0)])
        b = newbank()
        for j in range(4):
            pg.op("pe", lambda e, b=b, j=j: e.transpose(
                out=ps[b][:, j * P:(j + 1) * P], in_=gv[0][:, j * P:(j + 1) * P],
                identity=ident_f[:, :]), reads=[("gv", 0), "ident_f"], writes=[psr(b)])
        pg.op("dve", lambda e, b=b: e.tensor_copy(
            out=Ch[:, :, :], in_=ps[b][:, :].rearrange("p (j t) -> p j t", t=P)),
            reads=[psr(b)], writes=["Ch"])
        pg.op("dve", lambda e: e.tensor_scalar(out=LGh[:, :], in0=LGh[:, :], scalar1=0.5,
                                               scalar2=None, op0=ALU.mult),
              reads=["LGh"], writes=["LGh"])
        pg.op("dve", lambda e: e.tensor_scalar(out=negc[:, :], in0=negc[:, :], scalar1=-1.0,
                                               scalar2=None, op0=ALU.mult),
              reads=["negc"], writes=["negc"])
        for w_ in range(2):
            for h0 in (0, 4):
                b = newbank()
                pg.op("pe", lambda e, b=b, w_=w_, h0=h0: e.matmul(
                    ps[b][:, :], lhsT=J_f, rhs=Hk[:, w_, h0:h0 + 4, :], start=True, stop=True),
                    reads=["J_f", ("Hk", w_)], writes=[psr(b)])
                for hh in range(4):
                    h = h0 + hh
                    pg.op("act", lambda e, b=b, hh=hh, h=h, w_=w_: e.activation(
                        out=E[:, h, 1 + w_, :], in_=ps[b][:, hh * P:(hh + 1) * P], func=AF.Exp,
                        bias=negc[:, h:h + 1], scale=1.0),
                        reads=[psr(b), "negc"], writes=[("E", 1 + w_)])
        pool_ms(E[64:128, :, 2, 0:64], 0.0, [("E", 2)])

        cnt = {"xin": 0, "hn": 0, "xres": 0, "st4": 0, "bn": 0, "gv": 0, "van": 0, "mix": 0,
               "pt": 0, "ybt": 0, "tht": 0, "rd": 0}

        def nxt(name, n):
            v = cnt[name] % n
            cnt[name] += 1
            return v

        def rsqrt_chain(src_ap, src_res, scale, st_slot):
            r = ("st4", st_slot)
            pg.op("dve", lambda e: e.tensor_scalar(
                out=st4[:, st_slot, 1:2], in0=src_ap, scalar1=scale, scalar2=EPS,
                op0=ALU.mult, op1=ALU.add), reads=[src_res], writes=[(r, 1)])
            pg.op("pool", lambda e: e.tensor_tensor(
                out=st4[:, st_slot, 2:3], in0=st4[:, st_slot, 1:2], in1=mhalf[:, :], op=ALU.pow),
                reads=[(r, 1), "mhalf"], writes=[(r, 2)])
            return (r, 2)

        def in1_dma(x_src_rows, ntile):
            xslots = []
            for t in range(ntile):
                xs_ = nxt("xin", NXIN)
                dma("sp", xin[xs_][:, :], x_src_rows(t), [], [("xin", xs_)], f"xin{xs_}")
                xslots.append(xs_)
            return xslots

        def in1_compute(xslots):
            hslots = []
            for xs_ in xslots:
                hs_ = nxt("hn", 2)
                s4 = nxt("st4", NST4)
                pg.op("act", lambda e, xs_=xs_, hs_=hs_, s4=s4: e.activation(
                    out=hn[hs_][:, :], in_=xin[xs_][:, :], func=AF.Square,
                    accum_out=st4[:, s4, 0:1]),
                    reads=[("xin", xs_)], writes=[("hn", hs_), (("st4", s4), 0)])
                rr = rsqrt_chain(st4[:, s4, 0:1], (("st4", s4), 0), 1.0 / D, s4)
                pg.op("pool", lambda e, xs_=xs_, hs_=hs_, s4=s4: e.tensor_scalar(
                    out=hn[hs_][:, :], in0=xin[xs_][:, :], scalar1=st4[:, s4, 2:3], scalar2=1.0,
                    op0=ALU.mult, op1=ALU.mult),
                    reads=[("xin", xs_), rr], writes=[("hn", hs_)])
                hslots.append(hs_)
            return hslots

        def in1(x_src_rows, ntile):
            return in1_compute(in1_dma(x_src_rows, ntile))

        def tr_stage(hslots, w, ntile):
            banks = [newbank(), newbank()]
            for t, hs_ in enumerate(hslots):
                for k in range(KT):
                    b = banks[k // 4]
                    pg.op("pe", lambda e, b=b, k=k, t=t, hs_=hs_: e.transpose(
                        out=ps_bf(b)[:, (k % 4) * STK + t * P:(k % 4) * STK + (t + 1) * P],
                        in_=hn[hs_][:, k * P:(k + 1) * P], identity=ident_bf[:, :]),
                        reads=[("hn", hs_), "ident_bf"], writes=[psr(b)])
            n = ntile * P
            for k in range(KT):
                b = banks[k // 4]
                pg.op("dve", lambda e, b=b, k=k, n=n: e.tensor_scalar(
                    out=hT[:, k, 0:n], in0=ps_bf(b)[:, (k % 4) * STK:(k % 4) * STK + n],
                    scalar1=A2[:, k, w:w + 1], scalar2=mod[:, k, w:w + 1],
                    op0=ALU.mult, op1=ALU.add),
                    reads=[psr(b), "A2", ("mod", 0)], writes=[("hT", k)])

        def in_stage(x_src_rows, w, ntile):
            tr_stage(in1(x_src_rows, ntile), w, ntile)

        def t_matmul(blk, t):
            b = newbank()
            for k in range(KT):
                pg.op("pe", lambda e, b=b, k=k, t=t, blk=blk: e.matmul(
                    ps[b][:, :], lhsT=hT[:, k, t * P:(t + 1) * P],
                    rhs=w_in_bf[:, k, blk * 512:(blk + 1) * 512],
                    start=(k == 0), stop=(k == KT - 1)),
                    reads=[("hT", k), ("win", blk)], writes=[psr(b)])
            return b

        def f_matmul(blk, j0, n):
            b = newbank()
            for jj in range(2):
                c0 = blk * 512 + (j0 + jj) * P
                for k in range(KT):
                    pg.op("pe", lambda e, b=b, k=k, jj=jj, c0=c0, n=n: e.matmul(
                        ps[b][:, jj * STK:jj * STK + n], lhsT=w_in_bf[:, k, c0:c0 + P],
                        rhs=hT[:, k, 0:n], start=(k == 0), stop=(k == KT - 1)),
                        reads=[("hT", k), ("win", blk)], writes=[psr(b)])
            return b

        def bank3(b, n):
            return ps[b][:, :].rearrange("p (j t) -> p j t", t=STK)[:, :, 0:n]

        def v_evac(b, vslot, vbuf, vres, mask_ap, mask_res):
            src = ps[b][:, :].rearrange("p (h d) -> p h d", d=64)
            if mask_ap is None:
                pg.op("dve", lambda e: e.tensor_copy(out=vbuf[:, vslot, :, 0:64], in_=src),
                      reads=[psr(b)], writes=[(vres, vslot)])
                pg.op("pool", lambda e: e.memset(vbuf[:, vslot, :, 64:65], 1.0),
                      reads=[], writes=[(vres, vslot, "one")])
            else:
                pg.op("dve", lambda e: e.tensor_scalar(
                    out=vbuf[:, vslot, :, 0:64], in0=src, scalar1=mask_ap, scalar2=None,
                    op0=ALU.mult), reads=[psr(b), mask_res], writes=[(vres, vslot)])
                pg.op("dve", lambda e: e.tensor_copy(
                    out=vbuf[:, vslot, :, 64:65],
                    in_=mask_ap.unsqueeze(2).to_broadcast([P, 8, 1])),
                    reads=[mask_res], writes=[(vres, vslot, "one")])

        def att_make(t, keysrc):
            order = [1, 2, 0, 3]
            groups = []
            ybs = nxt("ybt", 2)

            def qk_group(g):
                pts = nxt("pt", NPT)
                bx = [newbank(), newbank()]
                by = newbank()
                for sl, kt in enumerate(order + [4]):
                    kap, kres, _, _ = keysrc[kt]
                    if kt == 4:
                        outap = ps[by][:, 0:2 * P]
                        wr = psr(by)
                    else:
                        outap = ps[bx[sl // 2]][:, (sl % 2) * 2 * P:(sl % 2 + 1) * 2 * P]
                        wr = psr(bx[sl // 2])
                    pg.op("pe", lambda e, outap=outap, kap=kap, g=g: e.matmul(
                        outap, lhsT=kap(g), rhs=qT2[:, g, :, t * P:(t + 1) * P],
                        start=True, stop=True),
                        reads=[kres(g), ("qT", g)], writes=[wr])
                for xb in range(2):
                    pg.op("act", lambda e, xb=xb, pts=pts, bx=bx: e.activation(
                        out=PT[pts][:, :, 2 * xb:2 * xb + 2, :].rearrange("p h s q -> p s h q"),
                        in_=ps[bx[xb]][:, :].rearrange("p (s h q) -> p s h q", h=2, q=P),
                        func=AF.Exp),
                        reads=[psr(bx[xb])], writes=[("PT", pts, xb)])
                pg.op("act", lambda e, pts=pts, by=by: e.activation(
                    out=PT[pts][:, :, 4, :],
                    in_=ps[by][:, 0:2 * P].rearrange("p (h q) -> p h q", q=P), func=AF.Exp),
                    reads=[psr(by)], writes=[("PT", pts, 2)])
                pg.op("pool", lambda e, pts=pts: e.memset(PT[pts][0:64, :, 2, 64:128], 0.0),
                      reads=[], writes=[("PT", pts, 1)])
                pg.op("dve", lambda e, pts=pts, g=g: e.tensor_tensor(
                    out=PT[pts][:, :, 3:5, :], in0=PT[pts][:, :, 3:5, :],
                    in1=E[:, 2 * g:2 * g + 2, 1:3, :], op=ALU.mult),
                    reads=[("PT", pts, 1), ("PT", pts, 2), ("E", 1), ("E", 2)],
                    writes=[("PT", pts, 1), ("PT", pts, 2)])
                return pts

            pv_state = {"bank": None}

            def pv_group(g, pts):
                if g % 2 == 0:
                    pv_state["bank"] = newbank()
                b = pv_state["bank"]
                for hh in range(2):
                    h = 2 * g + hh
                    col = (h % 4) * 65
                    for i, kt in enumerate(order + [4]):
                        _, _, vap, vres = keysrc[kt]
                        pg.op("pe", lambda e, b=b, col=col, pts=pts, hh=hh, i=i, vap=vap, h=h:
                              e.matmul(ps[b][:, col:col + 65], lhsT=PT[pts][:, hh, i, :],
                                       rhs=vap(h), start=(i == 0), stop=(i == 4)),
                              reads=[("PT", pts, 0), ("PT", pts, 1), ("PT", pts, 2)] + list(vres),
                              writes=[psr(b)])
                if g % 2 == 1:
                    hg = g // 2
                    r = nxt("rd", 2)
                    pv3 = ps[b][:, 0:260].rearrange("p (h d) -> p h d", d=65)
                    pg.op("dve", lambda e, r=r, pv3=pv3: e.reciprocal(
                        out=rdt[r][:, :], in_=pv3[:, :, 64]),
                        reads=[psr(b)], writes=[("rd", r)])
                    pg.op("dve", lambda e, r=r: e.tensor_scalar(
                        out=rdt[r][:, :], in0=rdt[r][:, :], scalar1=0.5, scalar2=None,
                        op0=ALU.mult), reads=[("rd", r)], writes=[("rd", r)])
                    pg.op("dve", lambda e, r=r, pv3=pv3, hg=hg: e.tensor_tensor(
                        out=yBt[ybs][:, hg * 256:(hg + 1) * 256].rearrange("p (h d) -> p h d", d=64),
                        in0=pv3[:, :, 0:64],
                        in1=rdt[r][:, :].unsqueeze(2).to_broadcast([P, 4, 64]), op=ALU.mult),
                        reads=[psr(b), ("rd", r)], writes=[("yBt", ybs, hg)])

            def yb_finish():
                b = newbank()
                for j in range(4):
                    pg.op("pe", lambda e, b=b, j=j: e.transpose(
                        out=ps_bf(b)[:, j * P:(j + 1) * P], in_=yBt[ybs][:, j * P:(j + 1) * P],
                        identity=ident_bf[:, :]),
                        reads=[("yBt", ybs, j // 2), "ident_bf"], writes=[psr(b)])
                pg.op("dve", lambda e, b=b: e.tensor_tensor(
                    out=o_inT[:, 4:8, t * P:(t + 1) * P],
                    in0=ps_bf(b)[:, 0:512].rearrange("p (j q) -> p j q", q=P),
                    in1=sgBT[:, :, t * P:(t + 1) * P], op=ALU.mult),
                    reads=[psr(b), "sgBT"], writes=[("o_inT", 1, t)])
            return qk_group, pv_group, yb_finish

        def attention_pair(t, keysrc, n_q):
            qk_group, pv_group, yb_finish = att_make(t, keysrc)
            pending = []
            for g in range(4):
                pts = qk_group(g)
                pending.append((g, pts))
                if len(pending) > 1:
                    pv_group(*pending.pop(0))
            while pending:
                pv_group(*pending.pop(0))
            yb_finish()

        def gmlp_tile(t, vs_):
            b = newbank()
            for g in range(8):
                pg.op("pe", lambda e, b=b, g=g: e.matmul(
                    ps[b][:, g * 64:(g + 1) * 64], lhsT=wsT[:, g, :],
                    rhs=vAn[vs_][:, g * 64:(g + 1) * 64], start=True, stop=True),
                    reads=["wsT", ("vAn", vs_)], writes=[psr(b)])
            ms_ = nxt("mix", 2)
            pg.op("dve", lambda e, b=b, ms_=ms_: e.tensor_tensor(
                out=mixb[ms_][:, :], in0=ps[b][:, :], in1=LGh[:, :], op=ALU.mult),
                reads=[psr(b), "LGh"], writes=[("mixb", ms_)])
            return ms_

        def gmlp_tile_finish(t, ms_):
            b = newbank()
            for j in range(4):
                pg.op("pe", lambda e, b=b, j=j: e.transpose(
                    out=ps_bf(b)[:, j * P:(j + 1) * P], in_=mixb[ms_][:, j * P:(j + 1) * P],
                    identity=ident_bf[:, :]), reads=[("mixb", ms_), "ident_bf"], writes=[psr(b)])
            pg.op("dve", lambda e, b=b: e.tensor_tensor(
                out=tmpA[:, :].rearrange("p (j t) -> p j t", t=P),
                in0=ps_bf(b)[:, 0:512].rearrange("p (j t) -> p j t", t=P),
                in1=Ch[:, :, :], op=ALU.add), reads=[psr(b), "Ch"], writes=["tmpA"])
            pg.op("dve", lambda e: e.tensor_tensor(
                out=o_inT[:, 0:4, t * P:(t + 1) * P],
                in0=tmpA[:, :].rearrange("p (j t) -> p j t", t=P),
                in1=guT[:, :, t * P:(t + 1) * P], op=ALU.mult),
                reads=["tmpA", "guT"], writes=[("o_inT", 0, t)])

        def xres_load(x_rows):
            xr = nxt("xres", NXRES)
            dma("sp", xres[xr][:, :], x_rows, [], [("xres", xr)], f"xres{xr}")
            return xr

        def out_mm(t):
            banks = []
            for hf in range(2):
                b = newbank()
                banks.append(b)
                for k in range(KT):
                    pg.op("pe", lambda e, b=b, k=k, hf=hf: e.matmul(
                        ps[b][:, :], lhsT=o_inT[:, k, t * P:(t + 1) * P],
                        rhs=w_out_bf[:, k, hf * 512:(hf + 1) * 512],
                        start=(k == 0), stop=(k == KT - 1)),
                        reads=[("o_inT", k // 4, t), ("wout", hf)], writes=[psr(b)])
            return banks

        def out_epi(banks, xr, y_rows, ggt=None, ggres="GG"):
            ggt = GG if ggt is None else ggt
            s4 = nxt("st4", NST4)
            for hf in range(2):
                b = banks[hf]
                pg.op("act", lambda e, b=b, hf=hf, s4=s4: e.activation(
                    out=tmpo[:, :], in_=ps[b][:, :], func=AF.Square,
                    accum_out=st4[:, s4, 2 * hf:2 * hf + 1] if hf == 0 else st4[:, s4, 3:4]),
                    reads=[psr(b)], writes=["tmpo", (("st4", s4), "s%d" % hf)])
            pg.op("dve", lambda e, s4=s4: e.tensor_tensor(
                out=st4[:, s4, 0:1], in0=st4[:, s4, 0:1], in1=st4[:, s4, 3:4], op=ALU.add),
                reads=[(("st4", s4), "s0"), (("st4", s4), "s1")], writes=[(("st4", s4), 0)])
            rr = rsqrt_chain(st4[:, s4, 0:1], (("st4", s4), 0), 1.0 / D, s4)
            for hf in range(2):
                b = banks[hf]
                pg.op("dve", lambda e, b=b, hf=hf, s4=s4: e.scalar_tensor_tensor(
                    out=tmpo[:, :], in0=ps[b][:, :], scalar=st4[:, s4, 2:3],
                    in1=ggt[:, hf * 512:(hf + 1) * 512], op0=ALU.mult, op1=ALU.mult),
                    reads=[psr(b), rr, (ggres, hf)], writes=["tmpo"])
                pg.op("pool", lambda e, hf=hf, xr=xr: e.tensor_tensor(
                    out=xres[xr][:, hf * 512:(hf + 1) * 512], in0=xres[xr][:, hf * 512:(hf + 1) * 512],
                    in1=tmpo[:, :], op=ALU.add),
                    reads=[("xres", xr), "tmpo"], writes=[("xres", xr)])
            dma("sp", y_rows, xres[xr][:, :], [("xres", xr)], [], f"xres{xr}")

        def out_tile(t, w, xr, y_rows):
            if w == 1:
                out_epi(out_mm(t), xr, y_rows, GGs, "GGs")
            else:
                out_epi(out_mm(t), xr, y_rows)

        def st_xrows(kind, s):
            if kind == "halo":
                base = (s + NHALO) * STK
                return lambda t: xh[base + t * P: base + (t + 1) * P, :]
            elif kind == "main":
                base = s * STK
                return lambda t: xq[base + t * P: base + (t + 1) * P, :]
            return lambda t: xs_d[:, :]

        def st_proj(kind, s):
            w = 1 if kind == "sample" else 0
            ntile = 1 if kind == "sample" else NTL
            n = ntile * P
            last = (kind == "main" and (s + 1) * STK > NTOK - LASTN)

            van_slots = []
            for t in range(ntile):
                gt = 2 * s + t
                vslot = (gt + 2 * NHALO) % RK
                if kind != "halo":
                    b = t_matmul(BLK_VA, t)
                    gs_ = nxt("gv", 2)
                    vs_ = nxt("van", 2)
                    bs_ = nxt("bn", 2)
                    s4 = nxt("st4", NST4)
                    pg.op("act", lambda e, b=b, gs_=gs_: e.activation(
                        out=gv[gs_][:, :], in_=ps[b][:, :], func=AF.Gelu_apprx_tanh),
                        reads=[psr(b)], writes=[("gv", gs_)])
                    pg.op("dve", lambda e, gs_=gs_, bs_=bs_: e.bn_stats(
                        out=bnst[bs_][:, :], in_=gv[gs_][:, :]),
                        reads=[("gv", gs_)], writes=[("bnst", bs_)])
                    pg.op("dve", lambda e, bs_=bs_: e.bn_aggr(out=bnmv[bs_][:, :], in_=bnst[bs_][:, :]),
                          reads=[("bnst", bs_)], writes=[("bnmv", bs_)])
                    rr = rsqrt_chain(bnmv[bs_][:, 1:2], ("bnmv", bs_), 1.0, s4)
                    pg.op("dve", lambda e, bs_=bs_, s4=s4: e.tensor_scalar(
                        out=st4[:, s4, 3:4], in0=bnmv[bs_][:, 0:1], scalar1=-1.0,
                        scalar2=st4[:, s4, 2:3], op0=ALU.mult, op1=ALU.mult),
                        reads=[("bnmv", bs_), rr], writes=[(("st4", s4), 3)])
                    if kind == "sample":
                        pg.op("act", lambda e, gs_=gs_, s4=s4: e.activation(
                            out=vAn_f, in_=gv[gs_][:, :], func=AF.Identity,
                            bias=st4[:, s4, 3:4], scale=st4[:, s4, 2:3]),
                            reads=[("gv", gs_), rr, (("st4", s4), 3)], writes=["vAn_f"])
                        pg.op("dve", lambda e, vs_=vs_: e.tensor_copy(out=vAn[vs_][:, :], in_=vAn_f),
                              reads=["vAn_f"], writes=[("vAn", vs_)])
                        pg.op("dve", lambda e: e.scalar_tensor_tensor(
                            out=gmv_s, in0=vAn_f, scalar=2.0, in1=LGh[:, :],
                            op0=ALU.mult, op1=ALU.mult), reads=["vAn_f", "LGh"], writes=["gmv_s"])
                        pg.op("pool", lambda e: e.tensor_tensor(
                            out=gmv_s, in0=gmv_s, in1=LBf[:, :], op=ALU.add),
                            reads=["gmv_s", "LBf"], writes=["gmv_s"])
                        dma("sp", gmv_d[:, :], gmv_s, ["gmv_s"], [], "o_gmv")
                    else:
                        pg.op("act", lambda e, gs_=gs_, vs_=vs_, s4=s4: e.activation(
                            out=vAn[vs_][:, :], in_=gv[gs_][:, :], func=AF.Identity,
                            bias=st4[:, s4, 3:4], scale=st4[:, s4, 2:3]),
                            reads=[("gv", gs_), rr, (("st4", s4), 3)], writes=[("vAn", vs_)])
                    van_slots.append(vs_)
                b = t_matmul(BLK_V, t)
                if kind == "halo":
                    v_evac(b, vslot, V, "V", hval[:, 0:1], "hval")
                elif kind == "main":
                    v_evac(b, vslot, V, "V", None, None)
                    if last:
                        r0 = s * STK + t * P - (NTOK - LASTN)
                        pg.op("act", lambda e, b=b: e.activation(out=nvst, in_=ps[b][:, :],
                                                                 func=AF.Identity),
                              reads=[psr(b)], writes=["nvst"])
                        dma("sp", nv_d[r0:r0 + P, :], nvst, ["nvst"], [], "o_nv")
                else:
                    v_evac(b, 4, Vs, "Vs", rmask[:, 0:1], "rmask")
                    pg.op("act", lambda e, b=b: e.activation(out=nvst, in_=ps[b][:, :],
                                                             func=AF.Identity),
                          reads=[psr(b)], writes=["nvst"])
                    dma("sp", nvs_d[:, :], nvst, ["nvst"], [], "o_nv")
                if last or kind == "sample":
                    b = t_matmul(BLK_K, t)
                    pg.op("act", lambda e, b=b: e.activation(out=nkst, in_=ps[b][:, :],
                                                             func=AF.Identity),
                          reads=[psr(b)], writes=["nkst"])
                    if kind == "sample":
                        dma("sp", nks_d[:, :], nkst, ["nkst"], [], "o_nk")
                    else:
                        r0 = s * STK + t * P - (NTOK - LASTN)
                        dma("sp", nk_d[r0:r0 + P, :], nkst, ["nkst"], [], "o_nk")

            for j0 in (0, 2):
                b = f_matmul(BLK_K, j0, n)
                if kind == "sample":
                    pg.op("dve", lambda e, b=b, j0=j0: e.tensor_copy(
                        out=kTs[:, j0:j0 + 2, 4 * P:5 * P], in_=bank3(b, n)),
                        reads=[psr(b)], writes=[("kTs", 4, j0), ("kTs", 4, j0 + 1)])
                else:
                    pos = ((2 * s + 2 * NHALO) % RK) * P
                    pg.op("dve", lambda e, b=b, j0=j0, pos=pos: e.tensor_copy(
                        out=kT[:, j0:j0 + 2, pos:pos + STK], in_=bank3(b, n)),
                        reads=[psr(b)],
                        writes=[("kT", (2 * s + tt + 2 * NHALO) % RK, j0 + jj)
                                for tt in range(2) for jj in range(2)])
            if kind == "halo":
                return van_slots
            for j0 in (0, 2):
                b = f_matmul(BLK_Q, j0, n)
                pg.op("dve", lambda e, b=b, j0=j0: e.tensor_scalar(
                    out=qT2[0:64, j0:j0 + 2, 0, 0:n], in0=bank3(b, n)[0:64], scalar1=0.125,
                    scalar2=None, op0=ALU.mult), reads=[psr(b)],
                    writes=[("qT", j0), ("qT", j0 + 1)])
                pg.op("dve", lambda e, b=b, j0=j0: e.tensor_scalar(
                    out=qT2[64:128, j0:j0 + 2, 1, 0:n], in0=bank3(b, n)[64:128], scalar1=0.125,
                    scalar2=None, op0=ALU.mult), reads=[psr(b)],
                    writes=[("qT", j0), ("qT", j0 + 1)])
            for j0 in (0, 2):
                b = f_matmul(BLK_U, j0, n)
                pg.op("act", lambda e, b=b, j0=j0: e.activation(
                    out=guT[:, j0:j0 + 2, 0:n], in_=bank3(b, n), func=AF.Gelu_apprx_tanh),
                    reads=[psr(b)], writes=["guT"])
            if kind == "main":
                pg.op("act", lambda e: e.activation(out=dout_[:, 0:1], in_=dzero[:, :], func=AF.Exp),
                      reads=["dzero"], writes=["dout0"])
            for blk, dst, dres in ((BLK_GA, sgAT, "sgAT"), (BLK_GB, sgBT, "sgBT")):
                for j0 in (0, 2):
                    b = f_matmul(blk, j0, n)
                    th = nxt("tht", 2)
                    thv = tht[th][:, :].rearrange("p (j t) -> p j t", t=STK)[:, :, 0:n]
                    pg.op("act", lambda e, b=b, thv=thv: e.activation(
                        out=thv, in_=bank3(b, n), func=AF.Tanh, scale=0.5),
                        reads=[psr(b)], writes=[("tht", th)])
                    pg.op("dve", lambda e, b=b, thv=thv, dst=dst, j0=j0: e.scalar_tensor_tensor(
                        out=dst[:, j0:j0 + 2, 0:n], in0=thv, scalar=1.0, in1=bank3(b, n),
                        op0=ALU.add, op1=ALU.mult),
                        reads=[psr(b), ("tht", th)], writes=[dres])
            pg.op("pool", lambda e: e.tensor_tensor(
                out=guT[:, :, 0:n], in0=guT[:, :, 0:n], in1=sgAT[:, :, 0:n], op=ALU.mult),
                reads=["guT", "sgAT"], writes=["guT"])

            return van_slots

        def st_keysrc(kind, s, t):
            keysrc = []
            if kind == "main":
                gt = 2 * s + t
                for kt in range(5):
                    sl = (gt - 4 + kt + 2 * NHALO) % RK
                    keysrc.append((
                        (lambda hp, sl=sl: kT[:, hp, sl * P:(sl + 1) * P]),
                        (lambda hp, sl=sl: ("kT", sl, hp)),
                        (lambda h, sl=sl: V[:, sl, h, :]),
                        (("V", sl), ("V", sl, "one"))))
            else:
                for kt in range(5):
                    vres = (("Vs", kt), ("Vs", kt, "one")) if kt == 4 else (("Vs", kt),)
                    keysrc.append((
                        (lambda hp, kt=kt: kTs[:, hp, kt * P:(kt + 1) * P]),
                        (lambda hp, kt=kt: ("kTs", kt, hp)),
                        (lambda h, kt=kt: Vs[:, kt, h, :]),
                        vres))
            return keysrc

        def gate_setup():
            bg = newbank()
            wada_mm(4, gate_slots[0], bg)
            wada_mm(5, gate_slots[1], bg)
            pg.op("dve", lambda e: e.tensor_tensor(
                out=mod[:, 16:24, :], in0=ps[bg][:, 32:48].rearrange("p (j w) -> p j w", w=2),
                in1=badac[:, 16:24].unsqueeze(2).to_broadcast([P, 8, 2]), op=ALU.add),
                reads=[psr(bg), "badac"], writes=[("mod", 1)])
            pg.op("dve", lambda e: e.tensor_tensor(
                out=GC[:, :, :], in0=mod[:, 16:24, :],
                in1=gpostc[:, :].unsqueeze(2).to_broadcast([P, 8, 2]), op=ALU.mult),
                reads=[("mod", 1), "gpostc"], writes=["GC"])
            make_GG(0)

        for s in range(-NHALO, 0):
            in_stage(st_xrows("halo", s), 0, NTL)
            st_proj("halo", s)

        def sample_prep():
            alias = [("wada", 0), ("wada", 1)]
            dma("pool", ck_bf, ck_d.rearrange("(t p) c -> p t c", p=P), [], ["ck_bf"] + alias, "c_ck")
            for i4 in range(4):
                dma("pool", Vs[:, i4, :, 0:64],
                    cv_d[i4 * P:(i4 + 1) * P, :].rearrange("p (h d) -> p h d", d=64), [],
                    [("Vs", i4)] + alias, f"c_cv{i4}")
            pg.op("pool", lambda e: e.memset(Vs[:, 0:4, :, 64:65], 1.0), reads=[],
                  writes=[("Vs", i) for i in range(4)])

        def sample_prep_pe():
            alias = [("wada", 0), ("wada", 1)]
            for hp in range(4):
                b = newbank()
                for kt in range(4):
                    pg.op("pe", lambda e, b=b, kt=kt, hp=hp: e.transpose(
                        out=ps_bf(b)[:, kt * P:(kt + 1) * P], in_=ck_bf[:, kt, hp * P:(hp + 1) * P],
                        identity=ident_bf[:, :]), reads=["ck_bf", "ident_bf"], writes=[psr(b)])
                pg.op("dve", lambda e, b=b, hp=hp: e.tensor_copy(
                    out=kTs[:, hp, 0:4 * P], in_=ps_bf(b)[:, 0:4 * P]),
                    reads=[psr(b)], writes=[("kTs", kt_, hp) for kt_ in range(4)] + alias)

        LAG = 3
        hs_cur = in1(st_xrows("main", 0), NTL)
        tr_stage(hs_cur, 0, NTL)
        for s in range(NST):
            xr = [xres_load(xq[s * STK + t * P: s * STK + (t + 1) * P, :]) for t in range(NTL)]
            if s + 1 < NST:
                xs_next = in1_dma(st_xrows("main", s + 1), NTL)
            elif sample:
                xs_next = in1_dma(st_xrows("sample", 0), 1)
            else:
                xs_next = None
            van_slots = st_proj("main", s)
            if s == 0:
                gate_setup()
            if s == min(2, NST - 1) and sample:
                sample_prep()
            if s == min(6, NST - 1) and sample:
                sample_prep_pe()
            if s == min(8, NST - 1) and sample:
                make_GG(1, GGs, "GGs")
            mix_slots = [None] * NTL
            atts = [att_make(t, st_keysrc("main", s, t)) for t in range(NTL)]
            stream = [(t, g) for t in range(NTL) for g in range(4)]
            pend = []
            hs_next = None
            for i, (t, g) in enumerate(stream):
                pts = atts[t][0](g)
                pend.append((t, g, pts))
                if i == 0:
                    mix_slots = [gmlp_tile(t2, van_slots[t2]) for t2 in range(NTL)]
                if i == 2:
                    for t2 in range(NTL):
                        gmlp_tile_finish(t2, mix_slots[t2])
                if len(pend) > LAG:
                    t3, g3, p3 = pend.pop(0)
                    atts[t3][1](g3, p3)
            nfl = 0
            while pend:
                t3, g3, p3 = pend.pop(0)
                atts[t3][1](g3, p3)
                nfl += 1
                if nfl == 1:
                    atts[0][2]()
            pg.op("act", lambda e: e.activation(out=dout_[:, 1:2], in_=dzero[:, :],
                                                func=AF.Gelu_apprx_tanh),
                  reads=["dzero"], writes=["dout1"])
            if xs_next is not None:
                hs_next = in1_compute(xs_next)
            bk0 = out_mm(0)
            atts[1][2]()
            out_epi(bk0, xr[0], y_d[s * STK: s * STK + P, :])
            if s + 1 < NST:
                tr_stage(hs_next, 0, NTL)
            elif sample:
                tr_stage(hs_next, 1, 1)
            out_tile(1, 0, xr[1], y_d[s * STK + P: s * STK + 2 * P, :])

        def proc_sample():
            xr = xres_load(xs_d[:, :])
            van_slots = st_proj("sample", 0)
            ms_ = gmlp_tile(0, van_slots[0])
            gmlp_tile_finish(0, ms_)
            attention_pair(0, st_keysrc("sample", 0, 0), P)
            out_tile(0, 1, xr, ys_d[:, :])

        if sample:
            proc_sample()

        pg.finalize()
        eng_sems = {e: es.enter_context(nc.semaphore(f"sem_{e}")) for e in ("pe", "act", "dve", "pool")}
        chan_sems = {c: es.enter_context(nc.semaphore(f"ch_{c}")) for c in pg.chan_count}
        final = [(chan_sems[c], v) for c, v in pg.chan_count.items()]
        block = es.enter_context(nc.Block())

        @block.sync
        def _(e):
            pg.emit("sp", e, eng_sems, chan_sems, final_waits=final)

        @block.gpsimd
        def _(e):
            pg.emit("pool", e, eng_sems, chan_sems)

        @block.scalar
        def _(e):
            pg.emit("act", e, eng_sems, chan_sems)

        @block.vector
        def _(e):
            pg.emit("dve", e, eng_sems, chan_sems)

        @block.tensor
        def _(e):
            pg.emit("pe", e, eng_sems, chan_sems)
    return nc


def host_inputs(inp, NST=16, NHALO=2, n_cores=8):
    f = lambda a: np.ascontiguousarray(np.asarray(a, dtype=np.float32))
    xp = f(inp["x_prompt"])
    B, S, _ = xp.shape
    qn = NST * STK
    nq = S // qn
    xsm = f(inp["x_sample"])
    ck = f(inp["cache_attn_k"])[0]
    cv = f(inp["cache_attn_v"])[0]
    cpr = f(inp["c_prompt"])
    csm = f(inp["c_sample"])
    col = lambda v, n: np.ascontiguousarray(v.reshape(n, P).T)
    rb = f(inp["rel_bias"])[0]
    shared = {
        "w_ada": f(inp["w_ada"])[0],
        "b_ada_c": col(f(inp["b_ada"])[0], 24),
        "w_in": f(inp["w_in"])[0],
        "g_pre_c": col(f(inp["g_pre"])[0], 8),
        "g_post_c": col(f(inp["g_post"])[0], 8),
        "ln_g": f(inp["ln_g"])[0].reshape(1, 512),
        "ln_b": f(inp["ln_b"])[0].reshape(1, 512),
        "w_s": f(inp["w_s"])[0],
        "b_sT": np.ascontiguousarray(f(inp["b_s"])[0].T),
        "rbp": np.ascontiguousarray(np.pad(rb, ((0, 0), (0, 127)), mode="edge")),
        "rb_last": np.ascontiguousarray(rb[:, 256].reshape(1, 8)),
        "w_out": f(inp["w_out"])[0],
        "ident": np.eye(P, dtype=np.float32),
        "antiident": np.ascontiguousarray(np.eye(P, dtype=np.float32)[::-1]),
        "tril": np.tril(np.ones((P, P), dtype=np.float32)),
    }
    rmask = np.zeros((P, 1), np.float32)
    rmask[:xsm.shape[1]] = 1.0
    maps = []
    for c in range(n_cores):
        b, qd = c // nq, c % nq
        m = dict(shared)
        m["xq"] = np.ascontiguousarray(xp[b, qd * qn:(qd + 1) * qn])
        hal = np.zeros((NHALO * STK, D), np.float32)
        if qd > 0:
            hal[:] = xp[b, qd * qn - NHALO * STK: qd * qn]
        m["xh"] = hal
        m["hval"] = np.full((P, 1), 1.0 if qd > 0 else 0.0, np.float32)
        m["rmask"] = rmask
        xs = np.zeros((P, D), np.float32)
        xs[:xsm.shape[1]] = xsm[c]
        m["xs"] = xs
        m["ck"] = np.ascontiguousarray(ck[c].reshape(512, 512))
        m["cv"] = np.ascontiguousarray(cv[c].reshape(512, 512))
        cvv = np.zeros((P, 8, 2), np.float32)
        cvv[:, :, 0] = cpr[b].reshape(8, P).T
        cvv[:, :, 1] = csm[c].reshape(8, P).T
        m["cvec"] = np.ascontiguousarray(cvv.reshape(P, 16))
        maps.append(m)
    return maps


def assemble(results, B, S, NST=16, n_cores=8, dec_seq=16):
    qn = NST * STK
    nq = S // qn
    lastn = min(512, qn)
    yp = np.zeros((B, S, D), np.float32)
    ysm = np.zeros((n_cores, dec_seq, D), np.float32)
    nkp = np.zeros((1, B, lastn, 8, 64), np.float32)
    nvp = np.zeros((1, B, lastn, 8, 64), np.float32)
    nks = np.zeros((1, n_cores, dec_seq, 8, 64), np.float32)
    nvs = np.zeros((1, n_cores, dec_seq, 8, 64), np.float32)
    gm = np.zeros((1, n_cores, dec_seq, 512), np.float32)
    for c, r in enumerate(results):
        b, qd = c // nq, c % nq
        yp[b, qd * qn:(qd + 1) * qn] = r["y"]
        ysm[c] = r["ys"][:dec_seq]
        if qd == nq - 1:
            nkp[0, b] = r["nk"].reshape(lastn, 8, 64)
            nvp[0, b] = r["nv"].reshape(lastn, 8, 64)
        nks[0, c] = r["nks"][:dec_seq].reshape(dec_seq, 8, 64)
        nvs[0, c] = r["nvs"][:dec_seq].reshape(dec_seq, 8, 64)
        gm[0, c] = r["gmv"][:dec_seq]
    return (yp, ysm, nkp, nvp, nks, nvs, gm)


def kernel(**inputs):
    NST = 16
    maps = host_inputs(inputs, NST=NST)
    nc = build(NST=NST)
    res = run_bass_kernel_spmd(nc, maps, core_ids=list(range(8)))
    B, S, _ = np.asarray(inputs["x_prompt"]).shape
    return assemble(res.results, B, S, NST=NST)
```
